# Optimizing a Trainium2 kernel written in Bass

```python
import jax, jax.numpy as jnp
from jax import lax
import numpy as np

D_MODEL = 1024
BATCH = 2
SEQ = 8192
DEPTH = 1

CHUNK = 64
MIX_WIDTH = D_MODEL
A_WIDTH = MIX_WIDTH // 2
B_WIDTH = MIX_WIDTH - A_WIDTH
GMLP_BLOCK = 128
A_HEADS = 4
A_HEAD_DIM = A_WIDTH // A_HEADS
CONV_WIDTH = 31
B_GROUPS = 8
FFN_HIDDEN = ((8 * D_MODEL + 3 * 256 - 1) // (3 * 256)) * 256
IN_WIDTH = 2 * A_WIDTH + 2 * B_WIDTH
RMS_EPS = 1e-6
LN_EPS = 1e-5

kernel_name = "hybrid_gmlp_conformer_conv_block"


def rms_norm(x, g):
    x32 = x.astype(jnp.float32)
    y = x32 * lax.rsqrt(jnp.mean(x32 * x32, axis=-1, keepdims=True) + RMS_EPS)
    return (y * g.astype(jnp.float32)).astype(x.dtype)


def layer_norm(x, g, b):
    x32 = x.astype(jnp.float32)
    mu = jnp.mean(x32, axis=-1, keepdims=True)
    xc = x32 - mu
    var = jnp.mean(xc * xc, axis=-1, keepdims=True)
    y = xc * lax.rsqrt(var + LN_EPS)
    return (y * g.astype(jnp.float32) + b.astype(jnp.float32)).astype(x.dtype)


def gmlp_spatial_gating(z, ln_g, ln_b, w_s, b_s):
    z = jax.nn.gelu(z)
    u, v = jnp.split(z, 2, axis=-1)
    v = layer_norm(v, ln_g, ln_b)
    bsz, s, _ = v.shape
    vb = v.reshape(bsz, s // GMLP_BLOCK, GMLP_BLOCK, A_HEADS, A_HEAD_DIM)
    chunk_id = jnp.arange(GMLP_BLOCK) // CHUNK
    mask = chunk_id[None, :] <= chunk_id[:, None]
    w = jnp.where(mask[None], w_s, jnp.zeros((), w_s.dtype))
    sg = jnp.einsum('hij,bnjhc->bnihc', w, vb) + b_s.T[None, None, :, :, None]
    return u * sg.reshape(bsz, s, A_WIDTH)


def conformer_conv_module(z, conv_w, conv_b, ln_g, ln_b):
    a, g = jnp.split(z, 2, axis=-1)
    h = a * jax.nn.sigmoid(g)
    h = lax.conv_general_dilated(
        h, conv_w[:, None, :].astype(h.dtype),
        window_strides=(1,), padding=((CONV_WIDTH - 1, 0),),
        dimension_numbers=('NWC', 'WIO', 'NWC'),
        feature_group_count=B_WIDTH) + conv_b
    h = layer_norm(h, ln_g, ln_b)
    return jax.nn.silu(h)


def swiglu(x, w_gate, w_up, w_down):
    return jnp.einsum('bsf,fd->bsd', jax.nn.silu(jnp.einsum('bsd,df->bsf', x, w_gate)) * jnp.einsum('bsd,df->bsf', x, w_up), w_down)


def setup_inputs(seed: int = 0) -> dict:
    key = jax.random.key(seed)
    ks = jax.random.split(key, 20)
    f32 = jnp.float32
    d = D_MODEL
    inp = {}
    inp['x'] = jax.random.normal(ks[0], (BATCH, SEQ, d), f32)
    inp['norm1_g'] = 1.0 + 0.05 * jax.random.normal(ks[1], (d,), f32)
    inp['w_in'] = jax.random.normal(ks[2], (d, IN_WIDTH), f32) * d ** -0.5
    inp['gmlp_ln_g'] = 1.0 + 0.05 * jax.random.normal(ks[3], (A_WIDTH,), f32)
    inp['gmlp_ln_b'] = 0.02 * jax.random.normal(ks[4], (A_WIDTH,), f32)
    inp['gmlp_w_s'] = jax.random.normal(ks[5], (A_HEADS, GMLP_BLOCK, GMLP_BLOCK), f32) * GMLP_BLOCK ** -0.5
    inp['gmlp_b_s'] = 1.0 + 0.1 * jax.random.normal(ks[6], (A_HEADS, GMLP_BLOCK), f32)
    inp['conv_w'] = jax.random.normal(ks[7], (CONV_WIDTH, B_WIDTH), f32) * CONV_WIDTH ** -0.5
    inp['conv_b'] = 0.02 * jax.random.normal(ks[8], (B_WIDTH,), f32)
    inp['conv_ln_g'] = 1.0 + 0.05 * jax.random.normal(ks[9], (B_WIDTH,), f32)
    inp['conv_ln_b'] = 0.02 * jax.random.normal(ks[10], (B_WIDTH,), f32)
    inp['w_out'] = jax.random.normal(ks[11], (MIX_WIDTH, d), f32) * MIX_WIDTH ** -0.5
    inp['norm2_g'] = 1.0 + 0.05 * jax.random.normal(ks[12], (d,), f32)
    inp['w_gate'] = jax.random.normal(ks[13], (d, FFN_HIDDEN), f32) * d ** -0.5
    inp['w_up'] = jax.random.normal(ks[14], (d, FFN_HIDDEN), f32) * d ** -0.5
    inp['w_down'] = jax.random.normal(ks[15], (FFN_HIDDEN, d), f32) * FFN_HIDDEN ** -0.5
    inp['final_norm_g'] = 1.0 + 0.05 * jax.random.normal(ks[16], (d,), f32)
    return inp


def reference(x, norm1_g, w_in, gmlp_ln_g, gmlp_ln_b, gmlp_w_s, gmlp_b_s, conv_w, conv_b,
              conv_ln_g, conv_ln_b, w_out, norm2_g, w_gate, w_up, w_down, final_norm_g):
    h = x
    for _ in range(DEPTH):
        xn = rms_norm(h, norm1_g)
        z = jnp.einsum('bsd,de->bse', xn, w_in)
        z_a = z[..., :2 * A_WIDTH]
        z_b = z[..., 2 * A_WIDTH:]
        y_a = gmlp_spatial_gating(z_a, gmlp_ln_g, gmlp_ln_b, gmlp_w_s, gmlp_b_s)
        y_b = conformer_conv_module(z_b, conv_w, conv_b, conv_ln_g, conv_ln_b)
        y = jnp.concatenate([y_a, y_b], axis=-1)
        h = h + jnp.einsum('bse,ed->bsd', y, w_out)
        h = h + swiglu(rms_norm(h, norm2_g), w_gate, w_up, w_down)
    return rms_norm(h, final_norm_g)
```

```python
import bisect
import numpy as np
import concourse.bass as bass
import concourse.mybir as mybir
from concourse.bass_utils import run_bass_kernel_spmd

F32 = mybir.dt.float32
BF16 = mybir.dt.bfloat16
I32 = mybir.dt.int32
AF = mybir.ActivationFunctionType
ALU = mybir.AluOpType
AX = mybir.AxisListType

NCORES = 8
D = 1024
TOK = 2048
NT = TOK // 128
NG = 4
FH = 2816
NFC = FH // 128
PORTIONS = [2, 2, 3, 3, 3, 3, 3, 3]
NPC = max(PORTIONS)
HALO = 32
CW = 31
RMS_EPS = 1e-6
LN_EPS = 1e-5
GELU_C = 0.044715
GELU_S = 1.5957691216057308

C_BS, C_CB, C_LG, C_LB, C_CW, C_ID, C_END = 0, 4, 8, 12, 16, 16 + 4 * CW, 16 + 4 * CW + 128


class Res:
    __slots__ = ("name", "lw", "rd")

    def __init__(self, name):
        self.name = name
        self.lw = None
        self.rd = {}


class Op:
    __slots__ = ("eng", "fn", "deps", "seq", "is_dma", "semkey", "cum", "needed", "count", "waits")

    def __init__(self, eng, fn, seq):
        self.eng = eng
        self.fn = fn
        self.deps = []
        self.seq = seq
        self.is_dma = False
        self.semkey = None
        self.cum = 0
        self.needed = False
        self.count = 0
        self.waits = []


class Sched:
    ENGS = ("pe", "act", "dve", "pool", "sp")

    def __init__(self):
        self.ops = {e: [] for e in self.ENGS}
        self.seq = 0
        self.dma_sems = {}
        self.pending_bar = {e: [] for e in self.ENGS}
        self.all_dma = []

    def _mk(self, eng, fn, reads, writes, is_dma=False, semkey=None, after=()):
        op = Op(eng, fn, self.seq)
        self.seq += 1
        op.is_dma = is_dma
        op.semkey = semkey
        deps = [(a, "raw") for a in after]
        if self.pending_bar[eng]:
            deps.extend(self.pending_bar[eng])
            self.pending_bar[eng] = []
        for r in reads:
            if r.lw is not None:
                deps.append((r.lw, "raw"))
        for w in writes:
            if w.lw is not None:
                deps.append((w.lw, "waw"))
            for o in w.rd.values():
                deps.append((o, "war"))
        for (o, kind) in deps:
            if o is op:
                continue
            if (not o.is_dma) and o.eng == eng and not is_dma:
                if eng == "pe" or kind == "war":
                    continue
            op.deps.append(o)
        key = ("dma:" + semkey) if is_dma else eng
        for r in reads:
            r.rd[key] = op
        for w in writes:
            w.lw = op
            w.rd = {}
        if is_dma:
            lst = self.dma_sems.setdefault(semkey, [])
            op.cum = (lst[-1][1] if lst else 0) + 16
            lst.append((op.seq, op.cum))
            self.all_dma.append(op)
        self.ops[eng].append(op)
        return op

    def op(self, eng, fn, reads=(), writes=(), after=()):
        return self._mk(eng, fn, reads, writes, after=after)

    def dma(self, eng, out, in_, semkey, reads=(), writes=(), after=()):
        return self._mk(eng, lambda e: e.dma_start(out=out, in_=in_), reads, writes, True, semkey, after=after)

    def barrier(self):
        toks = []
        for e in self.ENGS:
            comp = [o for o in self.ops[e] if not o.is_dma]
            if comp:
                toks.append((comp[-1], "bar"))
        for o in self.all_dma:
            toks.append((o, "bar"))
        last = {}
        for o in self.all_dma:
            last[o.semkey] = o
        toks = [t for t in toks if (not t[0].is_dma) or last[t[0].semkey] is t[0]]
        for e in self.ENGS:
            self.pending_bar[e] = list(toks)

    def finalize(self):
        for e in self.ENGS:
            for o in self.ops[e]:
                for d in o.deps:
                    if not d.is_dma:
                        d.needed = True
        for e in self.ENGS:
            c = 0
            for o in self.ops[e]:
                if o.is_dma:
                    continue
                if o.needed:
                    c += 1
                o.count = c
        for e in self.ENGS:
            seen = {}
            for o in self.ops[e]:
                need = {}
                for d in o.deps:
                    if d.is_dma:
                        lst = self.dma_sems[d.semkey]
                        i = bisect.bisect_left(lst, (o.seq, 0)) - 1
                        val = lst[i][1]
                        key = "dma:" + d.semkey
                    else:
                        key, val = d.eng, d.count
                    if val > need.get(key, 0):
                        need[key] = val
                for key, val in need.items():
                    if val > seen.get(key, 0):
                        seen[key] = val
                        o.waits.append((key, val))

    def emit(self, nc, block, sems):
        eng_attr = {"pe": "tensor", "act": "scalar", "dve": "vector", "pool": "gpsimd", "sp": "sync"}
        for e in self.ENGS:
            ops = self.ops[e]

            def body(eng, ops=ops, e=e):
                for o in ops:
                    for key, val in o.waits:
                        eng.wait_ge(sems[key], val)
                    ins = o.fn(eng)
                    if o.is_dma:
                        ins.then_inc(sems["dma:" + o.semkey], 16)
                    elif o.needed:
                        ins.then_inc(sems[e], 1)
                mine = {}
                for o in ops:
                    if o.is_dma:
                        mine[o.semkey] = max(mine.get(o.semkey, 0), o.cum)
                for k, v in mine.items():
                    eng.wait_ge(sems["dma:" + k], self.dma_sems[k][-1][1])

            getattr(block, eng_attr[e])(body)


def build_program():
    nc = bass.Bass("TRN2", target_bir_lowering=False)
    S = Sched()

    def din(name, shape, dt=F32):
        return nc.dram_tensor(name, list(shape), dt, kind="ExternalInput").ap()

    x_d = din("x", [TOK, D])
    xh_d = din("xh", [HALO, D])
    win_d = din("w_in", [D, 2048])
    wout_d = din("w_out", [D, D])
    wg_d = din("w_gate", [D, FH])
    wu_d = din("w_up", [D, FH])
    wd_d = din("w_down", [FH, D])
    rows_d = din("rows", [128, 3 * D + 1024])
    cpk_d = din("cpk", [128, C_END])
    wsn_d = din("wsn", [128, 512])
    wst_d = din("wst", [128, 512])
    y_d = nc.dram_tensor("y", [TOK, D], F32, kind="ExternalOutput").ap()

    x_v = x_d.rearrange("(t p) d -> p t d", p=128)
    y_v = y_d.rearrange("(t p) d -> p t d", p=128)
    win_v = win_d.rearrange("(k p) n -> p k n", p=128)
    wout_v = wout_d.rearrange("(k p) n -> p k n", p=128)
    wg_v = wg_d.rearrange("(k p) f -> p k f", p=128)
    wu_v = wu_d.rearrange("(k p) f -> p k f", p=128)
    wd_v = wd_d.rearrange("(c p) d -> p c d", p=128)

    from contextlib import ExitStack
    outer = ExitStack()
    ARENA_BYTES = 212000
    bump = {"off": 0, "max": 0}

    def sb(stack, name, shape, dt):
        esz = 4 if dt == F32 else 2
        n = 1
        for d_ in shape[1:]:
            n *= d_
        nbytes = (n * esz + 63) // 64 * 64
        off = bump["off"]
        bump["off"] = off + nbytes
        bump["max"] = max(bump["max"], bump["off"])
        assert bump["off"] <= ARENA_BYTES, (name, bump["off"])
        ap = arena[:, off // 2:(off + n * esz) // 2]
        if dt == F32:
            ap = ap.bitcast(F32)
        if len(shape) == 3:
            ap = ap.rearrange("p (a b) -> p a b", a=shape[1])
        return ap

    def ps(stack, name, shape, dt=F32):
        n = shape[1]
        off = bump["ps"]
        bump["ps"] = off + n
        assert bump["ps"] <= 4096
        return PSALL[:, off:off + n]

    with outer:
        arena = outer.enter_context(nc.sbuf_tensor("arena", [128, ARENA_BYTES // 2], BF16))
        PSALL = outer.enter_context(nc.psum_tensor("PSALL", [128, 4096], F32))
        bump["ps"] = 0
        xres = sb(outer, "xres", [128, NT, D], F32)
        r_x = [Res(f"x{t}") for t in range(NT)]
        cpk = sb(outer, "cpk", [128, C_END], F32)
        r_cpk = Res("cpk")
        identb = sb(outer, "identb", [128, 128], BF16)
        r_identb = Res("identb")
        dg = sb(outer, "dg", [128, 4 * CW, 128], BF16)
        r_dg = [Res(f"dg{k}") for k in range(4)]
        wsT = sb(outer, "wsT", [128, 4, 128], BF16)
        r_wsT = Res("wsT")
        gmg = sb(outer, "gmg", [128, 512], F32)
        r_gmg = Res("gmg")
        bias_a = sb(outer, "bias_a", [128, 512], F32)
        r_bias = Res("bias_a")
        stats = sb(outer, "stats", [128, 256], F32)
        r_stats = Res("stats")

        identf = cpk[:, C_ID:C_END]

        def stat_slice(c0, n, name):
            return stats[:, c0:c0 + n], Res(name)

        ss1, r_ss1 = stat_slice(0, 16, "ss1")
        r1, r_r1 = stat_slice(16, 16, "r1")
        ss2, r_ss2 = stat_slice(32, 16, "ss2")
        r2, r_r2 = stat_slice(48, 16, "r2")
        ss3, r_ss3 = stat_slice(64, 16, "ss3")
        r3, r_r3 = stat_slice(80, 16, "r3")
        rs_s, r_rs = stat_slice(96, 4, "rs")
        hst, r_hst = stat_slice(100, 8, "hst")
        tmpA, r_tmpA = stat_slice(108, 12, "tmpA")
        tmpB, r_tmpB = stat_slice(120, 12, "tmpB")
        tmpC, r_tmpC = stat_slice(132, 12, "tmpC")
        tmpD, r_tmpD = stat_slice(196, 48, "tmpD")
        lnv = [(stats[:, 144 + 13 * i:157 + 13 * i], Res(f"lnv{i}")) for i in range(2)]
        lnc = [(stats[:, 170 + 13 * i:183 + 13 * i], Res(f"lnc{i}")) for i in range(2)]

        sems = {}
        sem_stack = outer

        def getsem(key):
            if key not in sems:
                sems[key] = sem_stack.enter_context(nc.semaphore("s_" + key.replace(":", "_")))
            return sems[key]

        for e in Sched.ENGS:
            getsem(e)

        MAGIC = 1597463007.0

        def rsqrt_dve(a_ap, r_a, eps, y_ap, r_y, tmp_ap, r_tmp, n, iters=2):
            ae, ah, tt = tmp_ap[:, 0:n], tmp_ap[:, n:2 * n], tmp_ap[:, 2 * n:3 * n]
            S.op("dve", lambda e: e.tensor_scalar(out=ae, in0=a_ap, scalar1=eps, scalar2=None, op0=ALU.add),
                 reads=[r_a], writes=[r_tmp])
            S.op("dve", lambda e: e.tensor_scalar(out=y_ap.bitcast(I32), in0=ae.bitcast(I32), scalar1=-0.5,
                                                  scalar2=MAGIC, op0=ALU.mult, op1=ALU.add),
                 reads=[r_tmp], writes=[r_y])
            S.op("dve", lambda e: e.tensor_scalar(out=ah, in0=ae, scalar1=-0.5, scalar2=None, op0=ALU.mult),
                 reads=[r_tmp], writes=[r_tmp])
            for _ in range(iters):
                if n == 1:
                    S.op("dve", lambda e: e.scalar_tensor_tensor(out=tt, in0=y_ap, scalar=ah, in1=y_ap,
                                                                 op0=ALU.mult, op1=ALU.mult),
                         reads=[r_y, r_tmp], writes=[r_tmp])
                else:
                    S.op("dve", lambda e: e.tensor_tensor(out=tt, in0=y_ap, in1=ah, op=ALU.mult),
                         reads=[r_y, r_tmp], writes=[r_tmp])
                    S.op("dve", lambda e: e.tensor_tensor(out=tt, in0=tt, in1=y_ap, op=ALU.mult),
                         reads=[r_y, r_tmp], writes=[r_tmp])
                S.op("dve", lambda e: e.scalar_tensor_tensor(out=y_ap, in0=tt, scalar=1.5, in1=y_ap,
                                                             op0=ALU.add, op1=ALU.mult),
                     reads=[r_y, r_tmp], writes=[r_y])

        S.dma("sp", cpk[:], cpk_d, "c0", writes=[r_cpk])
        S.op("dve", lambda e: e.memset(stats[:], 0.0),
             writes=[r_stats, r_ss1, r_r1, r_ss2, r_r2, r_ss3, r_r3, r_rs, r_hst, r_tmpA, r_tmpB, r_tmpC, r_tmpD]
             + [r for _, r in lnv] + [r for _, r in lnc])

        ph1 = ExitStack()
        base_off = bump["off"]
        with ph1:
            off_w = bump["off"]
            win = sb(ph1, "win", [128, 8, 2048], BF16)
            r_winA, r_winB = Res("winA"), Res("winB")
            xsT = sb(ph1, "xsT", [128, 8, 512], BF16)
            r_xsT = [Res(f"xsT{j}") for j in range(4)]
            off_after_w = bump["off"]
            bump["off"] = off_w
            wgb, wub, wdb = [], [], []
            for i in range(2):
                wgb.append(sb(ph1, f"wgb{i}", [128, 8, NPC * 128], BF16))
                wub.append(sb(ph1, f"wub{i}", [128, 8, NPC * 128], BF16))
                wdb.append(sb(ph1, f"wdb{i}", [128, NPC, D], BF16))
            g2row = sb(ph1, "g2row", [128, D], F32)
            r_g2 = Res("g2row")
            assert bump["off"] <= off_after_w
            bump["off"] = off_after_w
            r_wb = [Res(f"wb{i}") for i in range(2)]
            wout = sb(ph1, "wout", [128, 8, D], BF16)
            r_wout = Res("wout")
            g1row = sb(ph1, "g1row", [128, D], F32)
            r_g1 = Res("g1row")
            xs = sb(ph1, "xs", [128, D], BF16)
            r_xs = Res("xs")
            xsTh = sb(ph1, "xsTh", [128, 8, HALO], BF16)
            r_xsTh = Res("xsTh")
            gt = sb(ph1, "gt", [128, D], F32)
            r_gtu, r_gtv = Res("gtu"), Res("gtv")
            vng = sb(ph1, "vng", [128, 512], BF16)
            r_vng = Res("vng")
            ya = [sb(ph1, f"ya{i}", [128, 512], BF16) for i in range(4)]
            r_ya = [Res(f"ya{i}") for i in range(4)]
            hT = [sb(ph1, f"hT{i}", [128, 4, HALO + 512], BF16) for i in range(2)]
            r_hT = [[Res(f"hT{i}_{k}") for k in range(4)] for i in range(2)]
            r_hTh = [Res(f"hTh{i}") for i in range(2)]
            sig = [sb(ph1, f"sig{i}", [128, 512], F32) for i in range(2)]
            r_sig = [Res(f"sig{i}") for i in range(2)]
            cT = sb(ph1, "cT", [128, 4, 512], F32)
            r_cT = [Res(f"cT{k}") for k in range(4)]
            cn_off = bump["off"]
            cn = [sb(ph1, f"cn{i}", [128, 512], BF16) for i in range(2)]
            r_cn = [Res(f"cn{i}") for i in range(2)]
            tb = sb(ph1, "tb", [128, 4, 128], F32)
            r_tb = Res("tb")
            sbg = sb(ph1, "sbg", [128, 4, 128], F32)
            r_sbg = Res("sbg")
            junk = sbg.rearrange("p a b -> p (a b)").bitcast(BF16)
            r_junk = r_sbg
            yTa = [sb(ph1, f"yTa{i}", [128, 4, 128], BF16) for i in range(4)]
            yTb = [sb(ph1, f"yTb{i}", [128, 4, 128], BF16) for i in range(2)]
            r_yTa = [Res(f"yTa{i}") for i in range(4)]
            r_yTb = [Res(f"yTb{i}") for i in range(2)]
            tsg = gt[:, 512:1024]
            wtmp, r_wtmp = gt[:, 512:1024], r_gtv

            P0 = ps(ph1, "P0", [128, 512])
            r_P0 = Res("P0")
            P0b = P0.bitcast(BF16)
            P1 = ps(ph1, "P1", [128, 512])
            r_P1 = Res("P1")
            ZA = ps(ph1, "ZA", [128, 1024])
            r_ZAh = [Res("ZA0"), Res("ZA1")]
            PO = ps(ph1, "PO", [128, 1024])
            r_PO = Res("PO")
            r_PO2 = Res("PO2")
            PAG = [ps(ph1, f"PAG{i}", [128, 512]) for i in range(2)]
            r_PAG = [Res(f"PAG{i}") for i in range(2)]

            def load_x(g, after=()):
                S.dma("sp", xres[:, 4 * g:4 * g + 4, :], x_v[:, 4 * g:4 * g + 4, :], f"x{g}",
                      writes=r_x[4 * g:4 * g + 4], after=after)

            S.dma("sp", g1row[:], rows_d[:, 0:D], "c0", writes=[r_g1])
            S.dma("sp", gt[0:HALO, :], xh_d, "c0", writes=[r_gtu, r_gtv])
            load_x(0)
            S.dma("pool", win[:, :, 0:1024], win_v[:, :, 0:1024], "winA", writes=[r_winA])
            S.dma("sp", gmg[:], rows_d[:, 3 * D:3 * D + 512], "c4", writes=[r_gmg])
            S.dma("sp", bias_a[:], rows_d[:, 3 * D + 512:3 * D + 1024], "c4", writes=[r_bias])
            deferred_loads = []

            S.op("dve", lambda e: e.tensor_scalar(out=cpk[:, C_LG:C_LB + 4], in0=cpk[:, C_LG:C_LB + 4], scalar1=0.5,
                                                  scalar2=None, op0=ALU.mult), reads=[r_cpk], writes=[r_cpk])
            S.op("dve", lambda e: e.tensor_copy(out=identb[:], in_=identf), reads=[r_cpk], writes=[r_identb])
            def make_dg(k):
                id_b = identf.unsqueeze(1).broadcast_to([128, CW, 128])
                cw_b = cpk[:, C_CW + k * CW:C_CW + (k + 1) * CW].unsqueeze(2).broadcast_to([128, CW, 128])
                S.op("dve", lambda e: e.tensor_tensor(out=dg[:, k * CW:(k + 1) * CW, :], in0=id_b, in1=cw_b,
                                                      op=ALU.mult), reads=[r_cpk], writes=[r_dg[k]])

            def x_stats(g):
                first = None
                for t in range(4 * g, 4 * g + 4):
                    o = S.op("act", lambda e, t=t: e.activation(out=junk, in_=xres[:, t, :], func=AF.Square,
                                                                scale=1.0 / 32.0, accum_out=ss1[:, t:t + 1]),
                             reads=[r_x[t]], writes=[r_junk, r_ss1])
                    first = first or o
                rsqrt_dve(ss1[:, 4 * g:4 * g + 4], r_ss1, RMS_EPS, r1[:, 4 * g:4 * g + 4], r_r1, tmpA, r_tmpA, 4)
                return first

            xs2 = arena[:, cn_off // 2:cn_off // 2 + D]

            def xs_sel(alt):
                return (xs2, r_cn) if alt else (xs, [r_xs])

            def xs_make(src_ap, r_srcs, npart, rr_ap, r_rr, alt=False):
                xb, r_xb = xs_sel(alt)
                S.op("dve", lambda e: e.scalar_tensor_tensor(out=xb[0:npart, :], in0=src_ap, scalar=rr_ap,
                                                             in1=g1row[0:npart, :], op0=ALU.mult, op1=ALU.mult),
                     reads=list(r_srcs) + [r_rr, r_g1], writes=list(r_xb))

            def xs_T(npart, evac_out, evac_res, alt=False):
                Pb, r_Pb = (P1.bitcast(BF16), r_P1) if alt else (P0b, r_P0)
                xb, r_xb = xs_sel(alt)
                for dc in range(8):
                    S.op("pe", lambda e, dc=dc: e.transpose(
                        out=Pb[:, dc * 128:dc * 128 + npart], in_=xb[0:npart, dc * 128:(dc + 1) * 128],
                        identity=identb[0:npart, 0:npart]),
                         reads=list(r_xb) + [r_identb], writes=[r_Pb])
                p0v = Pb.rearrange("p (c t) -> p c t", c=8)[:, :, 0:npart]
                S.op("act", lambda e: e.copy(out=evac_out, in_=p0v), reads=[r_Pb], writes=[evac_res])

            def xs_transpose(src_ap, r_srcs, npart, rr_ap, r_rr, evac_out, evac_res, alt=False):
                xs_make(src_ap, r_srcs, npart, rr_ap, r_rr, alt=alt)
                xs_T(npart, evac_out, evac_res, alt=alt)

            def xsT_make(g, j, alt=False):
                t = 4 * g + j
                xs_make(xres[:, t, :], [r_x[t]], 128, r1[:, t:t + 1], r_r1, alt=alt)

            def xsT_T(g, j, alt=False):
                xs_T(128, xsT[:, :, j * 128:(j + 1) * 128], r_xsT[j], alt=alt)

            def xsT_tile(g, j, alt=False):
                xsT_make(g, j, alt=alt)
                xsT_T(g, j, alt=alt)

            def glu_chunk(k, rhs_fn, rhs_res, ncols, out_ap, out_res, alt=False):
                if alt:
                    pa, pg, r_pa, r_pg = PO[:, 0:512], PO[:, 512:1024], r_PO, r_PO2
                else:
                    pa, pg, r_pa, r_pg = PAG[0], PAG[1], r_PAG[0], r_PAG[1]
                for (pt, r_pt, c0) in ((pa, r_pa, 1024 + 128 * k), (pg, r_pg, 1536 + 128 * k)):
                    for dc in range(8):
                        S.op("pe", lambda e, pt=pt, c0=c0, dc=dc: e.matmul(
                            pt[:, 0:ncols], lhsT=win[:, dc, c0:c0 + 128], rhs=rhs_fn(dc),
                            start=(dc == 0), stop=(dc == 7)),
                             reads=[r_winB] + rhs_res, writes=[r_pt])
                sg_t, r_sg_t = sig[k % 2], r_sig[k % 2]
                S.op("act", lambda e: e.activation(out=sg_t[:, 0:ncols], in_=pg[:, 0:ncols], func=AF.Tanh, scale=0.5),
                     reads=[r_pg], writes=[r_sg_t])
                S.op("dve", lambda e: e.scalar_tensor_tensor(out=out_ap, in0=sg_t[:, 0:ncols], scalar=1.0,
                                                             in1=pa[:, 0:ncols], op0=ALU.add, op1=ALU.mult),
                     reads=[r_pa, r_sg_t], writes=[out_res])

            def zb_chunk(g, k, alt=False):
                gb = g % 2
                glu_chunk(k, lambda dc: xsT[:, dc, :], r_xsT, 512, hT[gb][:, k, HALO:HALO + 512], r_hT[gb][k],
                          alt=alt)

            def halo_copy(g):
                gb = g % 2
                S.op("pool", lambda e: e.tensor_copy(out=hT[gb][:, :, 0:HALO], in_=hT[1 - gb][:, :, 512:512 + HALO]),
                     reads=r_hT[1 - gb], writes=[r_hTh[gb]])

            def za_mm(g, j):
                first = None
                for half in (1, 0):
                    for dc in range(8):
                        o = S.op("pe", lambda e, half=half, dc=dc: e.matmul(
                            ZA[:, half * 512:(half + 1) * 512], lhsT=xsT[:, dc, j * 128:(j + 1) * 128],
                            rhs=win[:, dc, half * 512:(half + 1) * 512], start=(dc == 0), stop=(dc == 7)),
                                 reads=[r_winA, r_xsT[j]], writes=[r_ZAh[half]])
                        first = first or o
                return first

            def gelu_ew(g, j):
                hv = [(gt[:, 512:1024], ZA[:, 512:1024], r_gtv, r_ZAh[1]), (gt[:, 0:512], ZA[:, 0:512], r_gtu, r_ZAh[0])]
                for (g_ap, z_ap, r_g, r_z) in hv:
                    S.op("act", lambda e, g_ap=g_ap, z_ap=z_ap: e.activation(out=g_ap, in_=z_ap,
                                                                             func=AF.Gelu_apprx_tanh),
                         reads=[r_z], writes=[r_g])

            def ln_stats(src_ap, r_src, st_ap, r_st):
                S.op("dve", lambda e: e.bn_stats(out=st_ap[:, 0:6], in_=src_ap), reads=[r_src], writes=[r_st])
                S.op("dve", lambda e: e.bn_aggr(out=st_ap[:, 6:8], in_=st_ap[:, 0:6]), reads=[r_st], writes=[r_st])
                rsqrt_dve(st_ap[:, 7:8], r_st, LN_EPS, st_ap[:, 8:9], r_st, st_ap[:, 9:12], r_st, 1, iters=2)

            def c_ln_sp(g, j):
                t = 4 * g + j
                st_ap, r_st = lnv[t % 2]
                v = gt[:, 512:1024]
                u = gt[:, 0:512]
                ln_stats(v, r_gtv, st_ap, r_st)
                S.op("dve", lambda e: e.tensor_scalar(out=v, in0=v, scalar1=st_ap[:, 6:7], scalar2=st_ap[:, 8:9],
                                                      op0=ALU.subtract, op1=ALU.mult),
                     reads=[r_gtv, r_st], writes=[r_gtv])
                S.op("dve", lambda e: e.tensor_tensor(out=vng[:], in0=v, in1=gmg[:], op=ALU.mult),
                     reads=[r_gtv, r_gmg], writes=[r_vng])
                for h in range(4):
                    S.op("pe", lambda e, h=h: e.matmul(P1[:, 128 * h:128 * h + 128], lhsT=wsT[:, h, :],
                                                       rhs=vng[:, 128 * h:128 * h + 128], start=True, stop=True),
                         reads=[r_wsT, r_vng], writes=[r_P1])
                S.op("dve", lambda e: e.tensor_tensor(out=tsg, in0=P1[:], in1=bias_a[:], op=ALU.add),
                     reads=[r_P1, r_bias, r_gtv], writes=[r_gtv])
                S.op("dve", lambda e: e.tensor_tensor(out=ya[j][:], in0=tsg, in1=u, op=ALU.mult),
                     reads=[r_gtu, r_gtv], writes=[r_ya[j]])

            def c_yaT(g, j):
                for h in range(4):
                    S.op("pe", lambda e, h=h: e.transpose(out=P0b[:, 128 * h:128 * h + 128],
                                                          in_=ya[j][:, 128 * h:128 * h + 128], identity=identb[:]),
                         reads=[r_ya[j], r_identb], writes=[r_P0])
                S.op("act", lambda e: e.copy(out=yTa[j][:],
                                             in_=P0b[:, 0:512].rearrange("p (c t) -> p c t", c=4)),
                     reads=[r_P0], writes=[r_yTa[j]])

            def conv_mm(g, k):
                gb = g % 2
                pc, r_pc = PAG[k % 2], r_PAG[k % 2]
                for tap in range(CW):
                    S.op("pe", lambda e, tap=tap: e.matmul(
                        pc[:], lhsT=dg[:, k * CW + tap, :], rhs=hT[gb][:, k, 2 + tap:2 + tap + 512],
                        start=(tap == 0), stop=(tap == CW - 1)),
                         reads=[r_dg[k], r_hT[gb][k], r_hTh[gb]], writes=[r_pc])

            def conv_evac(g, k):
                pc, r_pc = PAG[k % 2], r_PAG[k % 2]
                S.op("act", lambda e: e.activation(out=cT[:, k, :], in_=pc[:], func=AF.Identity,
                                                   bias=cpk[:, C_CB + k:C_CB + k + 1], scale=0.5),
                     reads=[r_pc, r_cpk], writes=[r_cT[k]])

            def conv_chunk(g, k):
                conv_mm(g, k)
                conv_evac(g, k)

            def stage_E1(g, j):
                t = 4 * g + j
                bk = j % 2
                pb, r_pb = (P1, r_P1) if bk == 0 else (ZA[:, 0:512], r_ZAh[0])
                for k in range(4):
                    S.op("pe", lambda e, k=k: e.transpose(out=pb[:, 128 * k:128 * k + 128],
                                                          in_=cT[:, k, j * 128:(j + 1) * 128], identity=identf),
                         reads=[r_cT[k], r_cpk], writes=[r_pb])
                st_ap, r_st = lnc[bk]
                ln_stats(pb, r_pb, st_ap, r_st)
                S.op("dve", lambda e: e.tensor_scalar(out=cn[bk][:], in0=pb, scalar1=st_ap[:, 6:7],
                                                      scalar2=st_ap[:, 8:9], op0=ALU.subtract, op1=ALU.mult),
                     reads=[r_pb, r_st], writes=[r_cn[bk]])

            def stage_E2(g, j):
                t = 4 * g + j
                tbi = j % 2
                for k in range(4):
                    S.op("pe", lambda e, k=k: e.transpose(out=P0b[:, 128 * k:128 * k + 128],
                                                          in_=cn[tbi][:, 128 * k:128 * k + 128], identity=identb[:]),
                         reads=[r_cn[tbi], r_identb], writes=[r_P0])
                for k in range(4):
                    S.op("act", lambda e, k=k: e.activation(
                        out=tb[:, k, :], in_=P0b[:, 128 * k:128 * k + 128], func=AF.Identity,
                        scale=cpk[:, C_LG + k:C_LG + k + 1], bias=cpk[:, C_LB + k:C_LB + k + 1]),
                         reads=[r_P0, r_cpk], writes=[r_tb])
                S.op("act", lambda e: e.activation(out=sbg[:], in_=tb[:], func=AF.Tanh),
                     reads=[r_tb], writes=[r_sbg])
                S.op("dve", lambda e: e.scalar_tensor_tensor(out=yTb[tbi][:], in0=sbg[:], scalar=1.0, in1=tb[:],
                                                             op0=ALU.add, op1=ALU.mult),
                     reads=[r_tb, r_sbg], writes=[r_yTb[tbi]])

            def stage_F(g, j):
                t = 4 * g + j
                tbi = j % 2
                for half in range(2):
                    for ec in range(8):
                        S.op("pe", lambda e, half=half, ec=ec: e.matmul(
                            PO[:, half * 512:(half + 1) * 512],
                            lhsT=(yTa[j][:, ec, :] if ec < 4 else yTb[tbi][:, ec - 4, :]),
                            rhs=wout[:, ec, half * 512:(half + 1) * 512], start=(ec == 0), stop=(ec == 7)),
                             reads=[r_wout, r_yTa[j], r_yTb[tbi]], writes=[r_PO, r_PO2])
                S.op("dve", lambda e: e.tensor_tensor(out=xres[:, t, :], in0=xres[:, t, :], in1=PO[:], op=ALU.add),
                     reads=[r_x[t], r_PO, r_PO2], writes=[r_x[t]])
                S.op("act", lambda e: e.activation(out=junk, in_=xres[:, t, :], func=AF.Square, scale=1.0 / 32.0,
                                                   accum_out=ss2[:, t:t + 1]),
                     reads=[r_x[t]], writes=[r_junk, r_ss2])

            S.op("act", lambda e: e.activation(out=junk[0:HALO, :], in_=gt[0:HALO, :], func=AF.Square,
                                               scale=1.0 / 32.0, accum_out=hst[0:HALO, 0:1]),
                 reads=[r_gtu, r_gtv], writes=[r_junk, r_hst])
            rsqrt_dve(hst[0:HALO, 0:1], r_hst, RMS_EPS, hst[0:HALO, 1:2], r_hst, hst[0:HALO, 2:5], r_hst, 1)
            xs_transpose(gt[0:HALO, :], [r_gtu, r_gtv], HALO, hst[0:HALO, 1:2], r_hst, xsTh[:], r_xsTh)
            make_dg(0)
            make_dg(1)
            o_first = x_stats(0)
            S.dma("pool", win[:, :, 1024:2048], win_v[:, :, 1024:2048], "winB", writes=[r_winB], after=[o_first])
            load_x(1, after=[o_first])
            for j in range(4):
                xsT_tile(0, j, alt=(j % 2 == 1))

            S.dma("sp", wtmp, wst_d, "c1", writes=[r_wtmp])
            wt3 = wtmp.rearrange("p (h i) -> p h i", h=4)
            S.op("dve", lambda e: e.memset(wt3[64:128, :, 0:64], 0.0), reads=[r_wtmp], writes=[r_wtmp])
            S.op("dve", lambda e: e.tensor_copy(out=wsT[:], in_=wt3), reads=[r_wtmp], writes=[r_wsT])
            S.dma("sp", wtmp, wsn_d, "c2", writes=[r_wtmp])
            S.op("dve", lambda e: e.memset(wt3[0:64, :, 64:128], 0.0), reads=[r_wtmp], writes=[r_wtmp])
            S.op("dve", lambda e: e.reduce_sum(out=rs_s, in_=wt3, axis=AX.X), reads=[r_wtmp], writes=[r_rs])
            for h in range(4):
                S.op("dve", lambda e, h=h: e.tensor_scalar(
                    out=bias_a[:, 128 * h:128 * h + 128], in0=bias_a[:, 128 * h:128 * h + 128],
                    scalar1=rs_s[:, h:h + 1], scalar2=cpk[:, C_BS + h:C_BS + h + 1],
                    op0=ALU.mult, op1=ALU.add), reads=[r_bias, r_rs, r_cpk], writes=[r_bias])

            offs = [0]
            for n_ in PORTIONS:
                offs.append(offs[-1] + n_)

            def load_portion(p, extra=()):
                b = p % 2
                n = PORTIONS[p]
                f0 = offs[p]
                wr = [r_wb[b]] + list(extra)
                S.dma("pool", wgb[b][:, :, 0:n * 128], wg_v[:, :, f0 * 128:(f0 + n) * 128], f"wb{b}", writes=wr)
                S.dma("pool", wub[b][:, :, 0:n * 128], wu_v[:, :, f0 * 128:(f0 + n) * 128], f"wb{b}", writes=wr)
                S.dma("pool", wdb[b][:, 0:n, :], wd_v[:, f0:f0 + n, :], f"wb{b}", writes=wr)

            for it in range(NG + 1):
                gY = it if it < NG else None
                gZ = it - 1 if it >= 1 else None
                gX = it + 1 if it + 1 < NG else None
                if gX is not None:
                    x_stats(gX)
                if gY is not None:
                    first = za_mm(gY, 0)
                    if gY == 0:
                        S.dma("pool", wout[:], wout_v, "wout", writes=[r_wout], after=[first])
                    if gY + 2 < NG:
                        load_x(gY + 2, after=[first])
                    gelu_ew(gY, 0)
                for j in range(4):
                    if gX is not None and it > 0:
                        xsT_make(gX, j)
                    if gZ is not None and not (it == NG and j < 2):
                        conv_chunk(gZ, j)
                    if it == 0:
                        zb_chunk(0, j)
                    if gX is not None and it > 0:
                        xsT_T(gX, j)
                    if gY is not None and j < 3:
                        za_mm(gY, j + 1)
                    if gY is not None:
                        c_ln_sp(gY, j)
                    if it == 0 and j < 2:
                        make_dg(2 + j)
                    if gY is not None and j < 3:
                        gelu_ew(gY, j + 1)
                if it == 0:
                    for k in range(4):
                        glu_chunk(k, lambda dc: xsTh[:, dc, :], [r_xsTh], HALO, hT[0][:, k, 0:HALO], r_hTh[0])
                    for j in range(4):
                        xsT_tile(gX, j, alt=(j % 2 == 1))
                if gX is not None:
                    halo_copy(gX)
                if gY == NG - 1:
                    load_portion(0, extra=[r_winA, r_winB] + r_xsT)
                    S.dma("sp", g2row[:], rows_d[:, D:2 * D], "c5", writes=[r_g2] + r_xsT)
                    load_portion(1, extra=[r_winA, r_winB] + r_xsT)
                for j in range(6):
                    if gZ is not None and j < 4:
                        stage_E1(gZ, j)
                    if gX is not None and j < 4:
                        zb_chunk(gX, j, alt=(gZ is None and j % 2 == 1))
                    if it == NG - 1 and j in (1, 2):
                        conv_mm(NG - 1, j - 1)
                    if gZ is not None and 1 <= j <= 4:
                        stage_E2(gZ, j - 1)
                    if it == NG - 1 and j == 3:
                        conv_evac(NG - 1, 0)
                        conv_evac(NG - 1, 1)
                    if gZ is not None and j >= 2:
                        stage_F(gZ, j - 2)
                    if gY is not None and j >= 2:
                        c_yaT(gY, j - 2)
                if gZ is not None:
                    rsqrt_dve(ss2[:, 4 * gZ:4 * gZ + 4], r_ss2, RMS_EPS, r2[:, 4 * gZ:4 * gZ + 4], r_r2,
                              tmpB, r_tmpB, 4)

        S.barrier()
        print("[kernel] phase1 arena bytes", bump["off"])
        bump["off"] = off_after_w
        bump["ps"] = 0
        ph2 = ExitStack()
        with ph2:
            hnT = sb(ph2, "hnT", [128, 8, TOK], BF16)
            r_hnT = [Res(f"hnT{t}") for t in range(NT)]
            g3row = sb(ph2, "g3row", [128, D], F32)
            r_g3 = Res("g3row")
            hn = [sb(ph2, f"hn{i}", [128, D], BF16) for i in range(2)]
            r_hn = [Res(f"hn{i}") for i in range(2)]
            junk2 = sb(ph2, "junk2", [128, D], BF16)
            r_junk2 = Res("junk2")
            actT = [sb(ph2, f"actT{i}", [128, NPC, 512], BF16) for i in range(2)]
            r_actT = [Res(f"actT{i}") for i in range(2)]
            sl = [sb(ph2, f"sl{i}", [128, 512], F32) for i in range(2)]
            r_sl = [Res(f"sl{i}") for i in range(2)]
            ot = [sb(ph2, f"ot{i}", [128, D], F32) for i in range(2)]
            r_ot = [Res(f"ot{i}") for i in range(2)]

            PG = [ps(ph2, f"PG{i}", [128, 512]) for i in range(2)]
            r_PG = [Res(f"PG{i}") for i in range(2)]
            PU = [ps(ph2, f"PU{i}", [128, 512]) for i in range(2)]
            r_PU = [Res(f"PU{i}") for i in range(2)]
            PD = [ps(ph2, f"PD{i}", [128, 1024]) for i in range(2)]
            r_PD = [Res(f"PD{i}") for i in range(2)]
            P0b2 = [PD[i][:, 0:512].bitcast(BF16) for i in range(2)]
            r_P02 = r_PD

            S.dma("sp", g3row[:], rows_d[:, 2 * D:3 * D], "c3", writes=[r_g3])

            for tt in range(NT):
                hb = tt % 2
                S.op("dve", lambda e, tt=tt, hb=hb: e.scalar_tensor_tensor(
                    out=hn[hb][:], in0=xres[:, tt, :], scalar=r2[:, tt:tt + 1], in1=g2row[:],
                    op0=ALU.mult, op1=ALU.mult), reads=[r_x[tt], r_r2, r_g2], writes=[r_hn[hb]])
                for dc in range(8):
                    S.op("pe", lambda e, dc=dc, hb=hb: e.transpose(out=P0b2[hb][:, dc * 128:(dc + 1) * 128],
                                                                   in_=hn[hb][:, dc * 128:(dc + 1) * 128],
                                                                   identity=identb[:]),
                         reads=[r_hn[hb], r_identb], writes=[r_P02[hb]])
                S.op("act", lambda e, tt=tt, hb=hb: e.copy(
                    out=hnT[:, :, tt * 128:(tt + 1) * 128],
                    in_=P0b2[hb].rearrange("p (c t) -> p c t", c=8)), reads=[r_P02[hb]], writes=[r_hnT[tt]])

            units = [(p, gi) for p in range(len(PORTIONS)) for gi in range(NG)]
            store_cnt = [0]

            def gu(p, gi, ui, fin=None, between=None):
                b = p % 2
                n = PORTIONS[p]
                ab = ui % 2
                for c in range(n):
                    pg_t, r_pg = PG[c % 2], r_PG[c % 2]
                    pu_t, r_pu = PU[c % 2], r_PU[c % 2]
                    for (wt, pt, r_pt) in ((wgb[b], pg_t, r_pg), (wub[b], pu_t, r_pu)):
                        for dc in range(8):
                            S.op("pe", lambda e, wt=wt, pt=pt, dc=dc, c=c: e.matmul(
                                pt[:], lhsT=wt[:, dc, c * 128:(c + 1) * 128],
                                rhs=hnT[:, dc, gi * 512:(gi + 1) * 512], start=(dc == 0), stop=(dc == 7)),
                                 reads=[r_wb[b]] + r_hnT[4 * gi:4 * gi + 4], writes=[r_pt])
                    sl_t, r_sl_t = sl[c % 2], r_sl[c % 2]
                    S.op("act", lambda e, pg_t=pg_t, sl_t=sl_t: e.activation(out=sl_t[:], in_=pg_t[:], func=AF.Silu),
                         reads=[r_pg], writes=[r_sl_t])
                    S.op("dve", lambda e, pu_t=pu_t, sl_t=sl_t, c=c: e.tensor_tensor(
                        out=actT[ab][:, c, :], in0=sl_t[:], in1=pu_t[:], op=ALU.mult),
                         reads=[r_sl_t, r_pu], writes=[r_actT[ab]])
                    if between is not None:
                        between(c)
                    if fin is not None:
                        if c == 0:
                            finalize_stats(fin)
                            finalize_tile(fin, 0)
                        elif c == 1:
                            finalize_tile(fin, 1)
                        if c == n - 1:
                            finalize_tile(fin, 2)
                            finalize_tile(fin, 3)

            def down(p, gi, ui, tiles=(0, 1, 2, 3)):
                b = p % 2
                n = PORTIONS[p]
                ab = ui % 2
                last = (p == len(PORTIONS) - 1)
                for j in tiles:
                    t = 4 * gi + j
                    db = t % 2
                    for half in range(2):
                        for c in range(n):
                            S.op("pe", lambda e, half=half, c=c, db=db, j=j: e.matmul(
                                PD[db][:, half * 512:(half + 1) * 512], lhsT=actT[ab][:, c, j * 128:(j + 1) * 128],
                                rhs=wdb[b][:, c, half * 512:(half + 1) * 512], start=(c == 0), stop=(c == n - 1)),
                                 reads=[r_wb[b], r_actT[ab]], writes=[r_PD[db]])
                    S.op("dve", lambda e, t=t, db=db: e.tensor_tensor(out=xres[:, t, :], in0=xres[:, t, :],
                                                                      in1=PD[db][:], op=ALU.add),
                         reads=[r_x[t], r_PD[db]], writes=[r_x[t]])
                    if last:
                        S.op("act", lambda e, t=t: e.activation(
                            out=junk2[:], in_=xres[:, t, :], func=AF.Square, scale=1.0 / 32.0,
                            accum_out=ss3[:, t:t + 1]), reads=[r_x[t]], writes=[r_junk2, r_ss3])

            def finalize_stats(gi):
                rsqrt_dve(ss3[:, 4 * gi:4 * gi + 4], r_ss3, RMS_EPS, r3[:, 4 * gi:4 * gi + 4], r_r3,
                          tmpC, r_tmpC, 4)

            def finalize_tile(gi, j, fast=True):
                t = 4 * gi + j
                ob = store_cnt[0] % 2
                store_cnt[0] += 1
                o_t, r_o = ot[ob], r_ot[ob]
                if fast:
                    S.op("dve", lambda e: e.scalar_tensor_tensor(
                        out=o_t[:], in0=xres[:, t, :], scalar=r3[:, t:t + 1], in1=g3row[:],
                        op0=ALU.mult, op1=ALU.mult), reads=[r_x[t], r_r3, r_g3], writes=[r_o])
                else:
                    S.op("act", lambda e: e.activation(out=o_t[:], in_=xres[:, t, :], func=AF.Copy,
                                                       scale=r3[:, t:t + 1]),
                         reads=[r_x[t], r_r3], writes=[r_o])
                    S.op("pool", lambda e: e.tensor_tensor(out=o_t[:], in0=o_t[:], in1=g3row[:], op=ALU.mult),
                         reads=[r_o, r_g3], writes=[r_o])
                S.dma("sp", y_v[:, t, :], o_t[:], f"st{ob}", reads=[r_o])

            def finalize(gi, fast=True):
                finalize_stats(gi)
                for j in range(4):
                    finalize_tile(gi, j, fast=fast)

            LASTP = len(PORTIONS) - 1
            pending = None
            for ui, (p, gi) in enumerate(units):
                btw = None
                if ui > 0:
                    pp, pgi = units[ui - 1]

                    def btw(c, pp=pp, pgi=pgi, ui=ui, n_cur=PORTIONS[p]):
                        split = {2: [(0, 1), (2, 3)], 3: [(0, 1), (2,), (3,)]}[n_cur]
                        down(pp, pgi, ui - 1, tiles=split[c])
                gu(p, gi, ui, fin=pending, between=btw)
                pending = None
                if ui > 0:
                    pp, pgi = units[ui - 1]
                    if pp == LASTP:
                        pending = pgi
                    if pgi == NG - 1 and pp + 2 < len(PORTIONS):
                        load_portion(pp + 2)
            pp, pgi = units[-1]
            down(pp, pgi, len(units) - 1)
            if pending is not None:
                finalize(pending)
            finalize(pgi, fast=True)

            print("[kernel] phase2 arena bytes", bump["off"])
            S.finalize()
            for k in S.dma_sems:
                getsem("dma:" + k)
            with nc.Block() as block:
                S.emit(nc, block, sems)
    return nc


_NC_CACHE = {}


def _get_nc():
    if "nc" not in _NC_CACHE:
        _NC_CACHE["nc"] = build_program()
    return _NC_CACHE["nc"]


def kernel(x, norm1_g, w_in, gmlp_ln_g, gmlp_ln_b, gmlp_w_s, gmlp_b_s, conv_w, conv_b,
           conv_ln_g, conv_ln_b, w_out, norm2_g, w_gate, w_up, w_down, final_norm_g):
    f = lambda a: np.ascontiguousarray(np.asarray(a, dtype=np.float32))
    x = f(x)
    B, SEQ, _ = x.shape
    xf = x.reshape(B * SEQ, D)
    cores_per_batch = NCORES // B

    def row(v):
        return np.broadcast_to(f(v)[None, :], (128, f(v).shape[0]))

    rows = np.ascontiguousarray(np.concatenate(
        [row(norm1_g), row(norm2_g), row(final_norm_g), row(gmlp_ln_g), row(gmlp_ln_b)], axis=1))
    cpk = np.zeros((128, C_END), np.float32)
    cpk[:, C_BS:C_BS + 4] = f(gmlp_b_s).T
    cpk[:, C_CB:C_CB + 4] = f(conv_b).reshape(4, 128).T
    cpk[:, C_LG:C_LG + 4] = f(conv_ln_g).reshape(4, 128).T
    cpk[:, C_LB:C_LB + 4] = f(conv_ln_b).reshape(4, 128).T
    cpk[:, C_CW:C_CW + 4 * CW] = f(conv_w).reshape(CW, 4, 128).transpose(2, 1, 0).reshape(128, 4 * CW)
    cpk[:, C_ID:C_END] = np.eye(128, dtype=np.float32)
    ws = f(gmlp_w_s)
    wsn = np.ascontiguousarray(ws.transpose(1, 0, 2).reshape(128, 512))
    wst = np.ascontiguousarray(ws.transpose(2, 0, 1).reshape(128, 512))

    shared = {"w_in": f(w_in), "w_out": f(w_out), "w_gate": f(w_gate), "w_up": f(w_up), "w_down": f(w_down),
              "rows": rows, "cpk": cpk, "wsn": wsn, "wst": wst}
    in_maps = []
    for c in range(NCORES):
        r0 = c * TOK
        if c % cores_per_batch == 0:
            xh = np.zeros((HALO, D), np.float32)
        else:
            xh = np.ascontiguousarray(xf[r0 - HALO:r0])
        m = dict(shared)
        m["x"] = np.ascontiguousarray(xf[r0:r0 + TOK])
        m["xh"] = xh
        in_maps.append(m)
    nc = _get_nc()
    res = run_bass_kernel_spmd(nc, in_maps, core_ids=list(range(NCORES)))
    out = np.concatenate([np.asarray(r["y"], dtype=np.float32) for r in res.results], axis=0)
    return out.reshape(B, SEQ, D)
```

```python
import bisect
import numpy as np
import concourse.bass as bass
import concourse.mybir as mybir
from concourse.bass_utils import run_bass_kernel_spmd

F32 = mybir.dt.float32
BF16 = mybir.dt.bfloat16
I32 = mybir.dt.int32
AF = mybir.ActivationFunctionType
ALU = mybir.AluOpType
AX = mybir.AxisListType

NCORES = 8
D = 1024
TOK = 2048
NT = TOK // 128
NG = 4
FH = 2816
NFC = FH // 128
PORTIONS = [2, 2, 3, 3, 3, 3, 3, 3]
NPC = max(PORTIONS)
HALO = 32
CW = 31
RMS_EPS = 1e-6
LN_EPS = 1e-5
GELU_C = 0.044715
GELU_S = 1.5957691216057308

C_BS, C_CB, C_LG, C_LB, C_CW, C_ID, C_END = 0, 4, 8, 12, 16, 16 + 4 * CW, 16 + 4 * CW + 128


class Res:
    __slots__ = ("name", "lw", "rd")

    def __init__(self, name):
        self.name = name
        self.lw = None
        self.rd = {}


class Op:
    __slots__ = ("eng", "fn", "deps", "seq", "is_dma", "semkey", "cum", "needed", "count", "waits")

    def __init__(self, eng, fn, seq):
        self.eng = eng
        self.fn = fn
        self.deps = []
        self.seq = seq
        self.is_dma = False
        self.semkey = None
        self.cum = 0
        self.needed = False
        self.count = 0
        self.waits = []


class Sched:
    ENGS = ("pe", "act", "dve", "pool", "sp")

    def __init__(self):
        self.ops = {e: [] for e in self.ENGS}
        self.seq = 0
        self.dma_sems = {}
        self.pending_bar = {e: [] for e in self.ENGS}
        self.all_dma = []

    def _mk(self, eng, fn, reads, writes, is_dma=False, semkey=None, after=()):
        op = Op(eng, fn, self.seq)
        self.seq += 1
        op.is_dma = is_dma
        op.semkey = semkey
        deps = [(a, "raw") for a in after]
        if self.pending_bar[eng]:
            deps.extend(self.pending_bar[eng])
            self.pending_bar[eng] = []
        for r in reads:
            if r.lw is not None:
                deps.append((r.lw, "raw"))
        for w in writes:
            if w.lw is not None:
                deps.append((w.lw, "waw"))
            for o in w.rd.values():
                deps.append((o, "war"))
        for (o, kind) in deps:
            if o is op:
                continue
            if (not o.is_dma) and o.eng == eng and not is_dma:
                if eng == "pe" or kind == "war":
                    continue
            op.deps.append(o)
        key = ("dma:" + semkey) if is_dma else eng
        for r in reads:
            r.rd[key] = op
        for w in writes:
            w.lw = op
            w.rd = {}
        if is_dma:
            lst = self.dma_sems.setdefault(semkey, [])
            op.cum = (lst[-1][1] if lst else 0) + 16
            lst.append((op.seq, op.cum))
            self.all_dma.append(op)
        self.ops[eng].append(op)
        return op

    def op(self, eng, fn, reads=(), writes=(), after=()):
        return self._mk(eng, fn, reads, writes, after=after)

    def dma(self, eng, out, in_, semkey, reads=(), writes=(), after=()):
        return self._mk(eng, lambda e: e.dma_start(out=out, in_=in_), reads, writes, True, semkey, after=after)

    def barrier(self):
        toks = []
        for e in self.ENGS:
            comp = [o for o in self.ops[e] if not o.is_dma]
            if comp:
                toks.append((comp[-1], "bar"))
        for o in self.all_dma:
            toks.append((o, "bar"))
        last = {}
        for o in self.all_dma:
            last[o.semkey] = o
        toks = [t for t in toks if (not t[0].is_dma) or last[t[0].semkey] is t[0]]
        for e in self.ENGS:
            self.pending_bar[e] = list(toks)

    def finalize(self):
        for e in self.ENGS:
            for o in self.ops[e]:
                for d in o.deps:
                    if not d.is_dma:
                        d.needed = True
        for e in self.ENGS:
            c = 0
            for o in self.ops[e]:
                if o.is_dma:
                    continue
                if o.needed:
                    c += 1
                o.count = c
        for e in self.ENGS:
            seen = {}
            for o in self.ops[e]:
                need = {}
                for d in o.deps:
                    if d.is_dma:
                        lst = self.dma_sems[d.semkey]
                        i = bisect.bisect_left(lst, (o.seq, 0)) - 1
                        val = lst[i][1]
                        key = "dma:" + d.semkey
                    else:
                        key, val = d.eng, d.count
                    if val > need.get(key, 0):
                        need[key] = val
                for key, val in need.items():
                    if val > seen.get(key, 0):
                        seen[key] = val
                        o.waits.append((key, val))

    def emit(self, nc, block, sems):
        eng_attr = {"pe": "tensor", "act": "scalar", "dve": "vector", "pool": "gpsimd", "sp": "sync"}
        for e in self.ENGS:
            ops = self.ops[e]

            def body(eng, ops=ops, e=e):
                for o in ops:
                    for key, val in o.waits:
                        eng.wait_ge(sems[key], val)
                    ins = o.fn(eng)
                    if o.is_dma:
                        ins.then_inc(sems["dma:" + o.semkey], 16)
                    elif o.needed:
                        ins.then_inc(sems[e], 1)
                mine = {}
                for o in ops:
                    if o.is_dma:
                        mine[o.semkey] = max(mine.get(o.semkey, 0), o.cum)
                for k, v in mine.items():
                    eng.wait_ge(sems["dma:" + k], self.dma_sems[k][-1][1])

            getattr(block, eng_attr[e])(body)


def build_program():
    nc = bass.Bass("TRN2", target_bir_lowering=False)
    S = Sched()

    def din(name, shape, dt=F32):
        return nc.dram_tensor(name, list(shape), dt, kind="ExternalInput").ap()

    x_d = din("x", [TOK, D])
    xh_d = din("xh", [HALO, D])
    win_d = din("w_in", [D, 2048])
    wout_d = din("w_out", [D, D])
    wg_d = din("w_gate", [D, FH])
    wu_d = din("w_up", [D, FH])
    wd_d = din("w_down", [FH, D])
    rows_d = din("rows", [128, 3 * D + 1024])
    cpk_d = din("cpk", [128, C_END])
    wsn_d = din("wsn", [128, 512])
    wst_d = din("wst", [128, 512])
    y_d = nc.dram_tensor("y", [TOK, D], F32, kind="ExternalOutput").ap()

    x_v = x_d.rearrange("(t p) d -> p t d", p=128)
    y_v = y_d.rearrange("(t p) d -> p t d", p=128)
    win_v = win_d.rearrange("(k p) n -> p k n", p=128)
    wout_v = wout_d.rearrange("(k p) n -> p k n", p=128)
    wg_v = wg_d.rearrange("(k p) f -> p k f", p=128)
    wu_v = wu_d.rearrange("(k p) f -> p k f", p=128)
    wd_v = wd_d.rearrange("(c p) d -> p c d", p=128)

    from contextlib import ExitStack
    outer = ExitStack()
    ARENA_BYTES = 212000
    bump = {"off": 0, "max": 0}

    def sb(stack, name, shape, dt):
        esz = 4 if dt == F32 else 2
        n = 1
        for d_ in shape[1:]:
            n *= d_
        nbytes = (n * esz + 63) // 64 * 64
        off = bump["off"]
        bump["off"] = off + nbytes
        bump["max"] = max(bump["max"], bump["off"])
        assert bump["off"] <= ARENA_BYTES, (name, bump["off"])
        ap = arena[:, off // 2:(off + n * esz) // 2]
        if dt == F32:
            ap = ap.bitcast(F32)
        if len(shape) == 3:
            ap = ap.rearrange("p (a b) -> p a b", a=shape[1])
        return ap

    def ps(stack, name, shape, dt=F32):
        n = shape[1]
        off = bump["ps"]
        bump["ps"] = off + n
        assert bump["ps"] <= 4096
        return PSALL[:, off:off + n]

    with outer:
        arena = outer.enter_context(nc.sbuf_tensor("arena", [128, ARENA_BYTES // 2], BF16))
        PSALL = outer.enter_context(nc.psum_tensor("PSALL", [128, 4096], F32))
        bump["ps"] = 0
        xres = sb(outer, "xres", [128, NT, D], F32)
        r_x = [Res(f"x{t}") for t in range(NT)]
        cpk = sb(outer, "cpk", [128, C_END], F32)
        r_cpk = Res("cpk")
        identb = sb(outer, "identb", [128, 128], BF16)
        r_identb = Res("identb")
        dg = sb(outer, "dg", [128, 4 * CW, 128], BF16)
        r_dg = [Res(f"dg{k}") for k in range(4)]
        wsT = sb(outer, "wsT", [128, 4, 128], BF16)
        r_wsT = Res("wsT")
        gmg = sb(outer, "gmg", [128, 512], F32)
        r_gmg = Res("gmg")
        bias_a = sb(outer, "bias_a", [128, 512], F32)
        r_bias = Res("bias_a")
        stats = sb(outer, "stats", [128, 256], F32)
        r_stats = Res("stats")

        identf = cpk[:, C_ID:C_END]

        def stat_slice(c0, n, name):
            return stats[:, c0:c0 + n], Res(name)

        ss1, r_ss1 = stat_slice(0, 16, "ss1")
        r1, r_r1 = stat_slice(16, 16, "r1")
        ss2, r_ss2 = stat_slice(32, 16, "ss2")
        r2, r_r2 = stat_slice(48, 16, "r2")
        ss3, r_ss3 = stat_slice(64, 16, "ss3")
        r3, r_r3 = stat_slice(80, 16, "r3")
        rs_s, r_rs = stat_slice(96, 4, "rs")
        hst, r_hst = stat_slice(100, 8, "hst")
        tmpA, r_tmpA = stat_slice(108, 12, "tmpA")
        tmpB, r_tmpB = stat_slice(120, 12, "tmpB")
        tmpC, r_tmpC = stat_slice(132, 12, "tmpC")
        tmpD, r_tmpD = stat_slice(196, 48, "tmpD")
        lnv = [(stats[:, 144 + 13 * i:157 + 13 * i], Res(f"lnv{i}")) for i in range(2)]
        lnc = [(stats[:, 170 + 13 * i:183 + 13 * i], Res(f"lnc{i}")) for i in range(2)]

        sems = {}
        sem_stack = outer

        def getsem(key):
            if key not in sems:
                sems[key] = sem_stack.enter_context(nc.semaphore("s_" + key.replace(":", "_")))
            return sems[key]

        for e in Sched.ENGS:
            getsem(e)

        MAGIC = 1597463007.0

        def rsqrt_dve(a_ap, r_a, eps, y_ap, r_y, tmp_ap, r_tmp, n, iters=2):
            ae, ah, tt = tmp_ap[:, 0:n], tmp_ap[:, n:2 * n], tmp_ap[:, 2 * n:3 * n]
            S.op("dve", lambda e: e.tensor_scalar(out=ae, in0=a_ap, scalar1=eps, scalar2=None, op0=ALU.add),
                 reads=[r_a], writes=[r_tmp])
            S.op("dve", lambda e: e.tensor_scalar(out=y_ap.bitcast(I32), in0=ae.bitcast(I32), scalar1=-0.5,
                                                  scalar2=MAGIC, op0=ALU.mult, op1=ALU.add),
                 reads=[r_tmp], writes=[r_y])
            S.op("dve", lambda e: e.tensor_scalar(out=ah, in0=ae, scalar1=-0.5, scalar2=None, op0=ALU.mult),
                 reads=[r_tmp], writes=[r_tmp])
            for _ in range(iters):
                if n == 1:
                    S.op("dve", lambda e: e.scalar_tensor_tensor(out=tt, in0=y_ap, scalar=ah, in1=y_ap,
                                                                 op0=ALU.mult, op1=ALU.mult),
                         reads=[r_y, r_tmp], writes=[r_tmp])
                else:
                    S.op("dve", lambda e: e.tensor_tensor(out=tt, in0=y_ap, in1=ah, op=ALU.mult),
                         reads=[r_y, r_tmp], writes=[r_tmp])
                    S.op("dve", lambda e: e.tensor_tensor(out=tt, in0=tt, in1=y_ap, op=ALU.mult),
                         reads=[r_y, r_tmp], writes=[r_tmp])
                S.op("dve", lambda e: e.scalar_tensor_tensor(out=y_ap, in0=tt, scalar=1.5, in1=y_ap,
                                                             op0=ALU.add, op1=ALU.mult),
                     reads=[r_y, r_tmp], writes=[r_y])

        S.dma("sp", cpk[:], cpk_d, "c0", writes=[r_cpk])
        S.op("dve", lambda e: e.memset(stats[:], 0.0),
             writes=[r_stats, r_ss1, r_r1, r_ss2, r_r2, r_ss3, r_r3, r_rs, r_hst, r_tmpA, r_tmpB, r_tmpC, r_tmpD]
             + [r for _, r in lnv] + [r for _, r in lnc])

        ph1 = ExitStack()
        base_off = bump["off"]
        with ph1:
            off_w = bump["off"]
            win = sb(ph1, "win", [128, 8, 2048], BF16)
            r_winA, r_winB = Res("winA"), Res("winB")
            xsT = sb(ph1, "xsT", [128, 8, 512], BF16)
            r_xsT = [Res(f"xsT{j}") for j in range(4)]
            off_after_w = bump["off"]
            bump["off"] = off_w
            wgb, wub, wdb = [], [], []
            for i in range(2):
                wgb.append(sb(ph1, f"wgb{i}", [128, 8, NPC * 128], BF16))
                wub.append(sb(ph1, f"wub{i}", [128, 8, NPC * 128], BF16))
                wdb.append(sb(ph1, f"wdb{i}", [128, NPC, D], BF16))
            g2row = sb(ph1, "g2row", [128, D], F32)
            r_g2 = Res("g2row")
            assert bump["off"] <= off_after_w
            bump["off"] = off_after_w
            r_wb = [Res(f"wb{i}") for i in range(2)]
            wout = sb(ph1, "wout", [128, 8, D], BF16)
            r_wout = Res("wout")
            g1row = sb(ph1, "g1row", [128, D], F32)
            r_g1 = Res("g1row")
            xs = sb(ph1, "xs", [128, D], BF16)
            r_xs = Res("xs")
            xsTh = sb(ph1, "xsTh", [128, 8, HALO], BF16)
            r_xsTh = Res("xsTh")
            gt = sb(ph1, "gt", [128, D], F32)
            r_gtu, r_gtv = Res("gtu"), Res("gtv")
            vng = sb(ph1, "vng", [128, 512], BF16)
            r_vng = Res("vng")
            ya = [sb(ph1, f"ya{i}", [128, 512], BF16) for i in range(4)]
            r_ya = [Res(f"ya{i}") for i in range(4)]
            hT = [sb(ph1, f"hT{i}", [128, 4, HALO + 512], BF16) for i in range(2)]
            r_hT = [[Res(f"hT{i}_{k}") for k in range(4)] for i in range(2)]
            r_hTh = [Res(f"hTh{i}") for i in range(2)]
            sig = [sb(ph1, f"sig{i}", [128, 512], F32) for i in range(2)]
            r_sig = [Res(f"sig{i}") for i in range(2)]
            cT = sb(ph1, "cT", [128, 4, 512], F32)
            r_cT = [Res(f"cT{k}") for k in range(4)]
            cn_off = bump["off"]
            cn = [sb(ph1, f"cn{i}", [128, 512], BF16) for i in range(2)]
            r_cn = [Res(f"cn{i}") for i in range(2)]
            tb = sb(ph1, "tb", [128, 4, 128], F32)
            r_tb = Res("tb")
            sbg = sb(ph1, "sbg", [128, 4, 128], F32)
            r_sbg = Res("sbg")
            junk = sbg.rearrange("p a b -> p (a b)").bitcast(BF16)
            r_junk = r_sbg
            yTa = [sb(ph1, f"yTa{i}", [128, 4, 128], BF16) for i in range(4)]
            yTb = [sb(ph1, f"yTb{i}", [128, 4, 128], BF16) for i in range(2)]
            r_yTa = [Res(f"yTa{i}") for i in range(4)]
            r_yTb = [Res(f"yTb{i}") for i in range(2)]
            tsg = gt[:, 512:1024]
            wtmp, r_wtmp = gt[:, 512:1024], r_gtv

            P0 = ps(ph1, "P0", [128, 512])
            r_P0 = Res("P0")
            P0b = P0.bitcast(BF16)
            P1 = ps(ph1, "P1", [128, 512])
            r_P1 = Res("P1")
            ZA = ps(ph1, "ZA", [128, 1024])
            r_ZAh = [Res("ZA0"), Res("ZA1")]
            PO = ps(ph1, "PO", [128, 1024])
            r_PO = Res("PO")
            r_PO2 = Res("PO2")
            PAG = [ps(ph1, f"PAG{i}", [128, 512]) for i in range(2)]
            r_PAG = [Res(f"PAG{i}") for i in range(2)]

            def load_x(g, after=()):
                S.dma("sp", xres[:, 4 * g:4 * g + 4, :], x_v[:, 4 * g:4 * g + 4, :], f"x{g}",
                      writes=r_x[4 * g:4 * g + 4], after=after)

            S.dma("sp", g1row[:], rows_d[:, 0:D], "c0", writes=[r_g1])
            S.dma("sp", gt[0:HALO, :], xh_d, "c0", writes=[r_gtu, r_gtv])
            load_x(0)
            S.dma("pool", win[:, :, 0:1024], win_v[:, :, 0:1024], "winA", writes=[r_winA])
            S.dma("sp", gmg[:], rows_d[:, 3 * D:3 * D + 512], "c4", writes=[r_gmg])
            S.dma("sp", bias_a[:], rows_d[:, 3 * D + 512:3 * D + 1024], "c4", writes=[r_bias])
            deferred_loads = []

            S.op("dve", lambda e: e.tensor_scalar(out=cpk[:, C_LG:C_LB + 4], in0=cpk[:, C_LG:C_LB + 4], scalar1=0.5,
                                                  scalar2=None, op0=ALU.mult), reads=[r_cpk], writes=[r_cpk])
            S.op("dve", lambda e: e.tensor_copy(out=identb[:], in_=identf), reads=[r_cpk], writes=[r_identb])
            def make_dg(k):
                id_b = identf.unsqueeze(1).broadcast_to([128, CW, 128])
                cw_b = cpk[:, C_CW + k * CW:C_CW + (k + 1) * CW].unsqueeze(2).broadcast_to([128, CW, 128])
                S.op("dve", lambda e: e.tensor_tensor(out=dg[:, k * CW:(k + 1) * CW, :], in0=id_b, in1=cw_b,
                                                      op=ALU.mult), reads=[r_cpk], writes=[r_dg[k]])

            def x_stats(g):
                first = None
                for t in range(4 * g, 4 * g + 4):
                    o = S.op("act", lambda e, t=t: e.activation(out=junk, in_=xres[:, t, :], func=AF.Square,
                                                                scale=1.0 / 32.0, accum_out=ss1[:, t:t + 1]),
                             reads=[r_x[t]], writes=[r_junk, r_ss1])
                    first = first or o
                rsqrt_dve(ss1[:, 4 * g:4 * g + 4], r_ss1, RMS_EPS, r1[:, 4 * g:4 * g + 4], r_r1, tmpA, r_tmpA, 4)
                return first

            xs2 = arena[:, cn_off // 2:cn_off // 2 + D]

            def xs_sel(alt):
                return (xs2, r_cn) if alt else (xs, [r_xs])

            def xs_make(src_ap, r_srcs, npart, rr_ap, r_rr, alt=False):
                xb, r_xb = xs_sel(alt)
                S.op("dve", lambda e: e.scalar_tensor_tensor(out=xb[0:npart, :], in0=src_ap, scalar=rr_ap,
                                                             in1=g1row[0:npart, :], op0=ALU.mult, op1=ALU.mult),
                     reads=list(r_srcs) + [r_rr, r_g1], writes=list(r_xb))

            def xs_T(npart, evac_out, evac_res, alt=False):
                Pb, r_Pb = (P1.bitcast(BF16), r_P1) if alt else (P0b, r_P0)
                xb, r_xb = xs_sel(alt)
                for dc in range(8):
                    S.op("pe", lambda e, dc=dc: e.transpose(
                        out=Pb[:, dc * 128:dc * 128 + npart], in_=xb[0:npart, dc * 128:(dc + 1) * 128],
                        identity=identb[0:npart, 0:npart]),
                         reads=list(r_xb) + [r_identb], writes=[r_Pb])
                p0v = Pb.rearrange("p (c t) -> p c t", c=8)[:, :, 0:npart]
                S.op("act", lambda e: e.copy(out=evac_out, in_=p0v), reads=[r_Pb], writes=[evac_res])

            def xs_transpose(src_ap, r_srcs, npart, rr_ap, r_rr, evac_out, evac_res, alt=False):
                xs_make(src_ap, r_srcs, npart, rr_ap, r_rr, alt=alt)
                xs_T(npart, evac_out, evac_res, alt=alt)

            def xsT_make(g, j, alt=False):
                t = 4 * g + j
                xs_make(xres[:, t, :], [r_x[t]], 128, r1[:, t:t + 1], r_r1, alt=alt)

            def xsT_T(g, j, alt=False):
                xs_T(128, xsT[:, :, j * 128:(j + 1) * 128], r_xsT[j], alt=alt)

            def xsT_tile(g, j, alt=False):
                xsT_make(g, j, alt=alt)
                xsT_T(g, j, alt=alt)

            def glu_chunk(k, rhs_fn, rhs_res, ncols, out_ap, out_res, alt=False):
                if alt:
                    pa, pg, r_pa, r_pg = PO[:, 0:512], PO[:, 512:1024], r_PO, r_PO2
                else:
                    pa, pg, r_pa, r_pg = PAG[0], PAG[1], r_PAG[0], r_PAG[1]
                for (pt, r_pt, c0) in ((pa, r_pa, 1024 + 128 * k), (pg, r_pg, 1536 + 128 * k)):
                    for dc in range(8):
                        S.op("pe", lambda e, pt=pt, c0=c0, dc=dc: e.matmul(
                            pt[:, 0:ncols], lhsT=win[:, dc, c0:c0 + 128], rhs=rhs_fn(dc),
                            start=(dc == 0), stop=(dc == 7)),
                             reads=[r_winB] + rhs_res, writes=[r_pt])
                sg_t, r_sg_t = sig[k % 2], r_sig[k % 2]
                S.op("act", lambda e: e.activation(out=sg_t[:, 0:ncols], in_=pg[:, 0:ncols], func=AF.Tanh, scale=0.5),
                     reads=[r_pg], writes=[r_sg_t])
                S.op("dve", lambda e: e.scalar_tensor_tensor(out=out_ap, in0=sg_t[:, 0:ncols], scalar=1.0,
                                                             in1=pa[:, 0:ncols], op0=ALU.add, op1=ALU.mult),
                     reads=[r_pa, r_sg_t], writes=[out_res])

            def zb_chunk(g, k, alt=False):
                gb = g % 2
                glu_chunk(k, lambda dc: xsT[:, dc, :], r_xsT, 512, hT[gb][:, k, HALO:HALO + 512], r_hT[gb][k],
                          alt=alt)

            def halo_copy(g):
                gb = g % 2
                S.op("pool", lambda e: e.tensor_copy(out=hT[gb][:, :, 0:HALO], in_=hT[1 - gb][:, :, 512:512 + HALO]),
                     reads=r_hT[1 - gb], writes=[r_hTh[gb]])

            def za_mm(g, j):
                first = None
                for half in (1, 0):
                    for dc in range(8):
                        o = S.op("pe", lambda e, half=half, dc=dc: e.matmul(
                            ZA[:, half * 512:(half + 1) * 512], lhsT=xsT[:, dc, j * 128:(j + 1) * 128],
                            rhs=win[:, dc, half * 512:(half + 1) * 512], start=(dc == 0), stop=(dc == 7)),
                                 reads=[r_winA, r_xsT[j]], writes=[r_ZAh[half]])
                        first = first or o
                return first

            def gelu_ew(g, j):
                hv = [(gt[:, 512:1024], ZA[:, 512:1024], r_gtv, r_ZAh[1]), (gt[:, 0:512], ZA[:, 0:512], r_gtu, r_ZAh[0])]
                for (g_ap, z_ap, r_g, r_z) in hv:
                    S.op("act", lambda e, g_ap=g_ap, z_ap=z_ap: e.activation(out=g_ap, in_=z_ap,
                                                                             func=AF.Gelu_apprx_tanh),
                         reads=[r_z], writes=[r_g])

            def ln_stats(src_ap, r_src, st_ap, r_st):
                S.op("dve", lambda e: e.bn_stats(out=st_ap[:, 0:6], in_=src_ap), reads=[r_src], writes=[r_st])
                S.op("dve", lambda e: e.bn_aggr(out=st_ap[:, 6:8], in_=st_ap[:, 0:6]), reads=[r_st], writes=[r_st])
                rsqrt_dve(st_ap[:, 7:8], r_st, LN_EPS, st_ap[:, 8:9], r_st, st_ap[:, 9:12], r_st, 1, iters=2)

            def c_ln_sp(g, j):
                t = 4 * g + j
                st_ap, r_st = lnv[t % 2]
                v = gt[:, 512:1024]
                u = gt[:, 0:512]
                ln_stats(v, r_gtv, st_ap, r_st)
                S.op("dve", lambda e: e.tensor_scalar(out=v, in0=v, scalar1=st_ap[:, 6:7], scalar2=st_ap[:, 8:9],
                                                      op0=ALU.subtract, op1=ALU.mult),
                     reads=[r_gtv, r_st], writes=[r_gtv])
                S.op("dve", lambda e: e.tensor_tensor(out=vng[:], in0=v, in1=gmg[:], op=ALU.mult),
                     reads=[r_gtv, r_gmg], writes=[r_vng])
                for h in range(4):
                    S.op("pe", lambda e, h=h: e.matmul(P1[:, 128 * h:128 * h + 128], lhsT=wsT[:, h, :],
                                                       rhs=vng[:, 128 * h:128 * h + 128], start=True, stop=True),
                         reads=[r_wsT, r_vng], writes=[r_P1])
                S.op("dve", lambda e: e.tensor_tensor(out=tsg, in0=P1[:], in1=bias_a[:], op=ALU.add),
                     reads=[r_P1, r_bias, r_gtv], writes=[r_gtv])
                S.op("dve", lambda e: e.tensor_tensor(out=ya[j][:], in0=tsg, in1=u, op=ALU.mult),
                     reads=[r_gtu, r_gtv], writes=[r_ya[j]])

            def c_yaT(g, j):
                for h in range(4):
                    S.op("pe", lambda e, h=h: e.transpose(out=P0b[:, 128 * h:128 * h + 128],
                                                          in_=ya[j][:, 128 * h:128 * h + 128], identity=identb[:]),
                         reads=[r_ya[j], r_identb], writes=[r_P0])
                S.op("act", lambda e: e.copy(out=yTa[j][:],
                                             in_=P0b[:, 0:512].rearrange("p (c t) -> p c t", c=4)),
                     reads=[r_P0], writes=[r_yTa[j]])

            def conv_mm(g, k):
                gb = g % 2
                pc, r_pc = PAG[k % 2], r_PAG[k % 2]
                for tap in range(CW):
                    S.op("pe", lambda e, tap=tap: e.matmul(
                        pc[:], lhsT=dg[:, k * CW + tap, :], rhs=hT[gb][:, k, 2 + tap:2 + tap + 512],
                        start=(tap == 0), stop=(tap == CW - 1)),
                         reads=[r_dg[k], r_hT[gb][k], r_hTh[gb]], writes=[r_pc])

            def conv_evac(g, k):
                pc, r_pc = PAG[k % 2], r_PAG[k % 2]
                S.op("act", lambda e: e.activation(out=cT[:, k, :], in_=pc[:], func=AF.Identity,
                                                   bias=cpk[:, C_CB + k:C_CB + k + 1], scale=0.5),
                     reads=[r_pc, r_cpk], writes=[r_cT[k]])

            def conv_chunk(g, k):
                conv_mm(g, k)
                conv_evac(g, k)

            def stage_E1(g, j):
                t = 4 * g + j
                bk = j % 2
                pb, r_pb = (P1, r_P1) if bk == 0 else (ZA[:, 0:512], r_ZAh[0])
                for k in range(4):
                    S.op("pe", lambda e, k=k: e.transpose(out=pb[:, 128 * k:128 * k + 128],
                                                          in_=cT[:, k, j * 128:(j + 1) * 128], identity=identf),
                         reads=[r_cT[k], r_cpk], writes=[r_pb])
                st_ap, r_st = lnc[bk]
                ln_stats(pb, r_pb, st_ap, r_st)
                S.op("dve", lambda e: e.tensor_scalar(out=cn[bk][:], in0=pb, scalar1=st_ap[:, 6:7],
                                                      scalar2=st_ap[:, 8:9], op0=ALU.subtract, op1=ALU.mult),
                     reads=[r_pb, r_st], writes=[r_cn[bk]])

            def stage_E2(g, j):
                t = 4 * g + j
                tbi = j % 2
                for k in range(4):
                    S.op("pe", lambda e, k=k: e.transpose(out=P0b[:, 128 * k:128 * k + 128],
                                                          in_=cn[tbi][:, 128 * k:128 * k + 128], identity=identb[:]),
                         reads=[r_cn[tbi], r_identb], writes=[r_P0])
                for k in range(4):
                    S.op("act", lambda e, k=k: e.activation(
                        out=tb[:, k, :], in_=P0b[:, 128 * k:128 * k + 128], func=AF.Identity,
                        scale=cpk[:, C_LG + k:C_LG + k + 1], bias=cpk[:, C_LB + k:C_LB + k + 1]),
                         reads=[r_P0, r_cpk], writes=[r_tb])
                S.op("act", lambda e: e.activation(out=sbg[:], in_=tb[:], func=AF.Tanh),
                     reads=[r_tb], writes=[r_sbg])
                S.op("dve", lambda e: e.scalar_tensor_tensor(out=yTb[tbi][:], in0=sbg[:], scalar=1.0, in1=tb[:],
                                                             op0=ALU.add, op1=ALU.mult),
                     reads=[r_tb, r_sbg], writes=[r_yTb[tbi]])

            def stage_F(g, j):
                t = 4 * g + j
                tbi = j % 2
                for half in range(2):
                    for ec in range(8):
                        S.op("pe", lambda e, half=half, ec=ec: e.matmul(
                            PO[:, half * 512:(half + 1) * 512],
                            lhsT=(yTa[j][:, ec, :] if ec < 4 else yTb[tbi][:, ec - 4, :]),
                            rhs=wout[:, ec, half * 512:(half + 1) * 512], start=(ec == 0), stop=(ec == 7)),
                             reads=[r_wout, r_yTa[j], r_yTb[tbi]], writes=[r_PO if half == 0 else r_PO2])
                S.op("dve", lambda e: e.tensor_tensor(out=xres[:, t, 0:512], in0=xres[:, t, 0:512], in1=PO[:, 0:512],
                                                      op=ALU.add), reads=[r_x[t], r_PO], writes=[r_x[t]])
                S.op("dve", lambda e: e.tensor_tensor(out=xres[:, t, 512:1024], in0=xres[:, t, 512:1024],
                                                      in1=PO[:, 512:1024], op=ALU.add),
                     reads=[r_x[t], r_PO2], writes=[r_x[t]])
                S.op("act", lambda e: e.activation(out=junk, in_=xres[:, t, :], func=AF.Square, scale=1.0 / 32.0,
                                                   accum_out=ss2[:, t:t + 1]),
                     reads=[r_x[t]], writes=[r_junk, r_ss2])

            S.op("act", lambda e: e.activation(out=junk[0:HALO, :], in_=gt[0:HALO, :], func=AF.Square,
                                               scale=1.0 / 32.0, accum_out=hst[0:HALO, 0:1]),
                 reads=[r_gtu, r_gtv], writes=[r_junk, r_hst])
            rsqrt_dve(hst[0:HALO, 0:1], r_hst, RMS_EPS, hst[0:HALO, 1:2], r_hst, hst[0:HALO, 2:5], r_hst, 1)
            xs_transpose(gt[0:HALO, :], [r_gtu, r_gtv], HALO, hst[0:HALO, 1:2], r_hst, xsTh[:], r_xsTh)
            make_dg(0)
            make_dg(1)
            o_first = x_stats(0)
            S.dma("pool", win[:, :, 1024:2048], win_v[:, :, 1024:2048], "winB", writes=[r_winB], after=[o_first])
            load_x(1, after=[o_first])
            for j in range(4):
                xsT_tile(0, j, alt=(j % 2 == 1))

            S.dma("sp", wtmp, wst_d, "c1", writes=[r_wtmp])
            wt3 = wtmp.rearrange("p (h i) -> p h i", h=4)
            S.op("dve", lambda e: e.memset(wt3[64:128, :, 0:64], 0.0), reads=[r_wtmp], writes=[r_wtmp])
            S.op("dve", lambda e: e.tensor_copy(out=wsT[:], in_=wt3), reads=[r_wtmp], writes=[r_wsT])
            S.dma("sp", wtmp, wsn_d, "c2", writes=[r_wtmp])
            S.op("dve", lambda e: e.memset(wt3[0:64, :, 64:128], 0.0), reads=[r_wtmp], writes=[r_wtmp])
            S.op("dve", lambda e: e.reduce_sum(out=rs_s, in_=wt3, axis=AX.X), reads=[r_wtmp], writes=[r_rs])
            for h in range(4):
                S.op("dve", lambda e, h=h: e.tensor_scalar(
                    out=bias_a[:, 128 * h:128 * h + 128], in0=bias_a[:, 128 * h:128 * h + 128],
                    scalar1=rs_s[:, h:h + 1], scalar2=cpk[:, C_BS + h:C_BS + h + 1],
                    op0=ALU.mult, op1=ALU.add), reads=[r_bias, r_rs, r_cpk], writes=[r_bias])

            offs = [0]
            for n_ in PORTIONS:
                offs.append(offs[-1] + n_)

            def load_portion(p, extra=()):
                b = p % 2
                n = PORTIONS[p]
                f0 = offs[p]
                wr = [r_wb[b]] + list(extra)
                S.dma("pool", wgb[b][:, :, 0:n * 128], wg_v[:, :, f0 * 128:(f0 + n) * 128], f"wb{b}", writes=wr)
                S.dma("pool", wub[b][:, :, 0:n * 128], wu_v[:, :, f0 * 128:(f0 + n) * 128], f"wb{b}", writes=wr)
                S.dma("pool", wdb[b][:, 0:n, :], wd_v[:, f0:f0 + n, :], f"wb{b}", writes=wr)

            for it in range(NG + 1):
                gY = it if it < NG else None
                gZ = it - 1 if it >= 1 else None
                gX = it + 1 if it + 1 < NG else None
                if gX is not None:
                    x_stats(gX)
                if gY is not None:
                    first = za_mm(gY, 0)
                    if gY == 0:
                        S.dma("pool", wout[:], wout_v, "wout", writes=[r_wout], after=[first])
                    if gY + 2 < NG:
                        load_x(gY + 2, after=[first])
                    gelu_ew(gY, 0)
                for j in range(4):
                    if gX is not None and it > 0:
                        xsT_make(gX, j)
                    if gZ is not None and not (it == NG and j < 2):
                        conv_chunk(gZ, j)
                    if it == 0:
                        zb_chunk(0, j)
                    if gX is not None and it > 0:
                        xsT_T(gX, j)
                    if gY is not None and j < 3:
                        za_mm(gY, j + 1)
                    if gY is not None:
                        c_ln_sp(gY, j)
                    if it == 0 and j < 2:
                        make_dg(2 + j)
                    if gY is not None and j < 3:
                        gelu_ew(gY, j + 1)
                if it == 0:
                    for k in range(4):
                        glu_chunk(k, lambda dc: xsTh[:, dc, :], [r_xsTh], HALO, hT[0][:, k, 0:HALO], r_hTh[0])
                    for j in range(4):
                        xsT_tile(gX, j, alt=(j % 2 == 1))
                if gX is not None:
                    halo_copy(gX)
                if gY == NG - 1:
                    load_portion(0, extra=[r_winA, r_winB] + r_xsT)
                    S.dma("sp", g2row[:], rows_d[:, D:2 * D], "c5", writes=[r_g2] + r_xsT)
                    load_portion(1, extra=[r_winA, r_winB] + r_xsT)
                for j in range(6):
                    if gZ is not None and j < 4:
                        stage_E1(gZ, j)
                    if gX is not None and j < 4:
                        zb_chunk(gX, j, alt=(gZ is None and j % 2 == 1))
                    if it == NG - 1 and j in (1, 2):
                        conv_mm(NG - 1, j - 1)
                    if gZ is not None and 1 <= j <= 4:
                        stage_E2(gZ, j - 1)
                    if it == NG - 1 and j == 3:
                        conv_evac(NG - 1, 0)
                        conv_evac(NG - 1, 1)
                    if gZ is not None and j >= 2:
                        stage_F(gZ, j - 2)
                    if gY is not None and j >= 2:
                        c_yaT(gY, j - 2)
                if gZ is not None:
                    rsqrt_dve(ss2[:, 4 * gZ:4 * gZ + 4], r_ss2, RMS_EPS, r2[:, 4 * gZ:4 * gZ + 4], r_r2,
                              tmpB, r_tmpB, 4)

        S.barrier()
        print("[kernel] phase1 arena bytes", bump["off"])
        bump["off"] = off_after_w
        bump["ps"] = 0
        ph2 = ExitStack()
        with ph2:
            hnT = sb(ph2, "hnT", [128, 8, TOK], BF16)
            r_hnT = [Res(f"hnT{t}") for t in range(NT)]
            g3row = sb(ph2, "g3row", [128, D], F32)
            r_g3 = Res("g3row")
            hn = [sb(ph2, f"hn{i}", [128, D], BF16) for i in range(2)]
            r_hn = [Res(f"hn{i}") for i in range(2)]
            junk2 = sb(ph2, "junk2", [128, D], BF16)
            r_junk2 = Res("junk2")
            actT = [sb(ph2, f"actT{i}", [128, NPC, 512], BF16) for i in range(2)]
            r_actT = [Res(f"actT{i}") for i in range(2)]
            sl = [sb(ph2, f"sl{i}", [128, 512], F32) for i in range(2)]
            r_sl = [Res(f"sl{i}") for i in range(2)]
            ot = [sb(ph2, f"ot{i}", [128, D], F32) for i in range(2)]
            r_ot = [Res(f"ot{i}") for i in range(2)]

            PG = [ps(ph2, f"PG{i}", [128, 512]) for i in range(2)]
            r_PG = [Res(f"PG{i}") for i in range(2)]
            PU = [ps(ph2, f"PU{i}", [128, 512]) for i in range(2)]
            r_PU = [Res(f"PU{i}") for i in range(2)]
            PD = [ps(ph2, f"PD{i}", [128, 1024]) for i in range(2)]
            r_PD = [Res(f"PD{i}") for i in range(2)]
            P0b2 = [PD[i][:, 0:512].bitcast(BF16) for i in range(2)]
            r_P02 = r_PD

            S.dma("sp", g3row[:], rows_d[:, 2 * D:3 * D], "c3", writes=[r_g3])

            for tt in range(NT):
                hb = tt % 2
                S.op("dve", lambda e, tt=tt, hb=hb: e.scalar_tensor_tensor(
                    out=hn[hb][:], in0=xres[:, tt, :], scalar=r2[:, tt:tt + 1], in1=g2row[:],
                    op0=ALU.mult, op1=ALU.mult), reads=[r_x[tt], r_r2, r_g2], writes=[r_hn[hb]])
                for dc in range(8):
                    S.op("pe", lambda e, dc=dc, hb=hb: e.transpose(out=P0b2[hb][:, dc * 128:(dc + 1) * 128],
                                                                   in_=hn[hb][:, dc * 128:(dc + 1) * 128],
                                                                   identity=identb[:]),
                         reads=[r_hn[hb], r_identb], writes=[r_P02[hb]])
                S.op("act", lambda e, tt=tt, hb=hb: e.copy(
                    out=hnT[:, :, tt * 128:(tt + 1) * 128],
                    in_=P0b2[hb].rearrange("p (c t) -> p c t", c=8)), reads=[r_P02[hb]], writes=[r_hnT[tt]])

            units = [(p, gi) for p in range(len(PORTIONS)) for gi in range(NG)]
            store_cnt = [0]

            def gu(p, gi, ui, fin=None, between=None):
                b = p % 2
                n = PORTIONS[p]
                ab = ui % 2
                for c in range(n):
                    pg_t, r_pg = PG[c % 2], r_PG[c % 2]
                    pu_t, r_pu = PU[c % 2], r_PU[c % 2]
                    for (wt, pt, r_pt) in ((wgb[b], pg_t, r_pg), (wub[b], pu_t, r_pu)):
                        for dc in range(8):
                            S.op("pe", lambda e, wt=wt, pt=pt, dc=dc, c=c: e.matmul(
                                pt[:], lhsT=wt[:, dc, c * 128:(c + 1) * 128],
                                rhs=hnT[:, dc, gi * 512:(gi + 1) * 512], start=(dc == 0), stop=(dc == 7)),
                                 reads=[r_wb[b]] + r_hnT[4 * gi:4 * gi + 4], writes=[r_pt])
                    sl_t, r_sl_t = sl[c % 2], r_sl[c % 2]
                    S.op("act", lambda e, pg_t=pg_t, sl_t=sl_t: e.activation(out=sl_t[:], in_=pg_t[:], func=AF.Silu),
                         reads=[r_pg], writes=[r_sl_t])
                    S.op("dve", lambda e, pu_t=pu_t, sl_t=sl_t, c=c: e.tensor_tensor(
                        out=actT[ab][:, c, :], in0=sl_t[:], in1=pu_t[:], op=ALU.mult),
                         reads=[r_sl_t, r_pu], writes=[r_actT[ab]])
                    if between is not None:
                        between(c)
                    if fin is not None:
                        if c == 0:
                            finalize_stats(fin)
                            finalize_tile(fin, 0)
                        elif c == 1:
                            finalize_tile(fin, 1)
                        if c == n - 1:
                            finalize_tile(fin, 2)
                            finalize_tile(fin, 3)

            def down(p, gi, ui, tiles=(0, 1, 2, 3)):
                b = p % 2
                n = PORTIONS[p]
                ab = ui % 2
                last = (p == len(PORTIONS) - 1)
                for j in tiles:
                    t = 4 * gi + j
                    db = t % 2
                    for half in range(2):
                        for c in range(n):
                            S.op("pe", lambda e, half=half, c=c, db=db, j=j: e.matmul(
                                PD[db][:, half * 512:(half + 1) * 512], lhsT=actT[ab][:, c, j * 128:(j + 1) * 128],
                                rhs=wdb[b][:, c, half * 512:(half + 1) * 512], start=(c == 0), stop=(c == n - 1)),
                                 reads=[r_wb[b], r_actT[ab]], writes=[r_PD[db]])
                    S.op("dve", lambda e, t=t, db=db: e.tensor_tensor(out=xres[:, t, :], in0=xres[:, t, :],
                                                                      in1=PD[db][:], op=ALU.add),
                         reads=[r_x[t], r_PD[db]], writes=[r_x[t]])
                    if last:
                        S.op("act", lambda e, t=t: e.activation(
                            out=junk2[:], in_=xres[:, t, :], func=AF.Square, scale=1.0 / 32.0,
                            accum_out=ss3[:, t:t + 1]), reads=[r_x[t]], writes=[r_junk2, r_ss3])

            def finalize_stats(gi):
                rsqrt_dve(ss3[:, 4 * gi:4 * gi + 4], r_ss3, RMS_EPS, r3[:, 4 * gi:4 * gi + 4], r_r3,
                          tmpC, r_tmpC, 4)

            def finalize_tile(gi, j, fast=True):
                t = 4 * gi + j
                ob = store_cnt[0] % 2
                store_cnt[0] += 1
                o_t, r_o = ot[ob], r_ot[ob]
                if fast:
                    S.op("dve", lambda e: e.scalar_tensor_tensor(
                        out=o_t[:], in0=xres[:, t, :], scalar=r3[:, t:t + 1], in1=g3row[:],
                        op0=ALU.mult, op1=ALU.mult), reads=[r_x[t], r_r3, r_g3], writes=[r_o])
                else:
                    S.op("act", lambda e: e.activation(out=o_t[:], in_=xres[:, t, :], func=AF.Copy,
                                                       scale=r3[:, t:t + 1]),
                         reads=[r_x[t], r_r3], writes=[r_o])
                    S.op("pool", lambda e: e.tensor_tensor(out=o_t[:], in0=o_t[:], in1=g3row[:], op=ALU.mult),
                         reads=[r_o, r_g3], writes=[r_o])
                S.dma("sp", y_v[:, t, :], o_t[:], f"st{ob}", reads=[r_o])

            def finalize(gi, fast=True):
                finalize_stats(gi)
                for j in range(4):
                    finalize_tile(gi, j, fast=fast)

            LASTP = len(PORTIONS) - 1
            pending = None
            for ui, (p, gi) in enumerate(units):
                btw = None
                if ui > 0:
                    pp, pgi = units[ui - 1]

                    def btw(c, pp=pp, pgi=pgi, ui=ui, n_cur=PORTIONS[p]):
                        split = {2: [(0, 1), (2, 3)], 3: [(0, 1), (2,), (3,)]}[n_cur]
                        down(pp, pgi, ui - 1, tiles=split[c])
                gu(p, gi, ui, fin=pending, between=btw)
                pending = None
                if ui > 0:
                    pp, pgi = units[ui - 1]
                    if pp == LASTP:
                        pending = pgi
                    if pgi == NG - 1 and pp + 2 < len(PORTIONS):
                        load_portion(pp + 2)
            pp, pgi = units[-1]
            down(pp, pgi, len(units) - 1)
            if pending is not None:
                finalize(pending)
            finalize(pgi, fast=True)

            print("[kernel] phase2 arena bytes", bump["off"])
            S.finalize()
            for k in S.dma_sems:
                getsem("dma:" + k)
            with nc.Block() as block:
                S.emit(nc, block, sems)
    return nc


_NC_CACHE = {}


def _get_nc():
    if "nc" not in _NC_CACHE:
        _NC_CACHE["nc"] = build_program()
    return _NC_CACHE["nc"]


def kernel(x, norm1_g, w_in, gmlp_ln_g, gmlp_ln_b, gmlp_w_s, gmlp_b_s, conv_w, conv_b,
           conv_ln_g, conv_ln_b, w_out, norm2_g, w_gate, w_up, w_down, final_norm_g):
    f = lambda a: np.ascontiguousarray(np.asarray(a, dtype=np.float32))
    x = f(x)
    B, SEQ, _ = x.shape
    xf = x.reshape(B * SEQ, D)
    cores_per_batch = NCORES // B

    def row(v):
        return np.broadcast_to(f(v)[None, :], (128, f(v).shape[0]))

    rows = np.ascontiguousarray(np.concatenate(
        [row(norm1_g), row(norm2_g), row(final_norm_g), row(gmlp_ln_g), row(gmlp_ln_b)], axis=1))
    cpk = np.zeros((128, C_END), np.float32)
    cpk[:, C_BS:C_BS + 4] = f(gmlp_b_s).T
    cpk[:, C_CB:C_CB + 4] = f(conv_b).reshape(4, 128).T
    cpk[:, C_LG:C_LG + 4] = f(conv_ln_g).reshape(4, 128).T
    cpk[:, C_LB:C_LB + 4] = f(conv_ln_b).reshape(4, 128).T
    cpk[:, C_CW:C_CW + 4 * CW] = f(conv_w).reshape(CW, 4, 128).transpose(2, 1, 0).reshape(128, 4 * CW)
    cpk[:, C_ID:C_END] = np.eye(128, dtype=np.float32)
    ws = f(gmlp_w_s)
    wsn = np.ascontiguousarray(ws.transpose(1, 0, 2).reshape(128, 512))
    wst = np.ascontiguousarray(ws.transpose(2, 0, 1).reshape(128, 512))

    shared = {"w_in": f(w_in), "w_out": f(w_out), "w_gate": f(w_gate), "w_up": f(w_up), "w_down": f(w_down),
              "rows": rows, "cpk": cpk, "wsn": wsn, "wst": wst}
    in_maps = []
    for c in range(NCORES):
        r0 = c * TOK
        if c % cores_per_batch == 0:
            xh = np.zeros((HALO, D), np.float32)
        else:
            xh = np.ascontiguousarray(xf[r0 - HALO:r0])
        m = dict(shared)
        m["x"] = np.ascontiguousarray(xf[r0:r0 + TOK])
        m["xh"] = xh
        in_maps.append(m)
    nc = _get_nc()
    res = run_bass_kernel_spmd(nc, in_maps, core_ids=list(range(NCORES)))
    out = np.concatenate([np.asarray(r["y"], dtype=np.float32) for r in res.results], axis=0)
    return out.reshape(B, SEQ, D)
```
